# Optimizing a Trainium2 kernel written in Bass

```python
import jax, jax.numpy as jnp
from jax import lax
import numpy as np

D_MODEL = 1024
BATCH = 8
SEQ = 4096
DEPTH = 4

PLE_DIM = 256
GROUP_WIDTH = D_MODEL // 8
FNET_GROUPS = 4
SGU_GROUPS = 4
CONF_GROUPS = 4
SCONV_GROUPS = 4
FNET_WIDTH = FNET_GROUPS * GROUP_WIDTH
SGU_WIDTH = SGU_GROUPS * GROUP_WIDTH
CONF_WIDTH = CONF_GROUPS * GROUP_WIDTH
SCONV_WIDTH = SCONV_GROUPS * GROUP_WIDTH
EVEN_IN = FNET_WIDTH + 2 * SGU_WIDTH
ODD_IN = 2 * CONF_WIDTH + 3 * SCONV_WIDTH
MIX_WIDTH = FNET_WIDTH + SGU_WIDTH
CHUNK = 128
CONF_KERNEL = 31
SCONV_KERNEL = 3
FFN_KERNEL = 3
D_FF = ((8 * D_MODEL // 3 + 127) // 128) * 128
EPS = 1e-6
N_EVEN = (DEPTH + 1) // 2
N_ODD = DEPTH // 2

kernel_name = "hybrid_fnet_sgu_conformer_shortconv_encoder"


def rmsnorm(x, g):
    xf = x.astype(jnp.float32)
    y = xf * lax.rsqrt(jnp.mean(xf * xf, axis=-1, keepdims=True) + EPS)
    return (y * g.astype(jnp.float32)).astype(x.dtype)


def layernorm(x, g, b):
    xf = x.astype(jnp.float32)
    mu = jnp.mean(xf, axis=-1, keepdims=True)
    xc = xf - mu
    var = jnp.mean(xc * xc, axis=-1, keepdims=True)
    y = xc * lax.rsqrt(var + EPS) * g.astype(jnp.float32) + b.astype(jnp.float32)
    return y.astype(x.dtype)


def depthwise_conv(x, w, b=None):
    k = w.shape[0]
    c = x.shape[-1]
    y = lax.conv_general_dilated(
        x, w[:, None, :].astype(x.dtype), window_strides=(1,),
        padding=[(k // 2, k // 2)], dimension_numbers=('NWC', 'WIO', 'NWC'),
        feature_group_count=c)
    if b is not None:
        y = y + b.astype(x.dtype)
    return y


def fourier_mixer(a, w_f):
    bn, s, _ = a.shape
    a4 = a.reshape(bn, s, FNET_GROUPS, GROUP_WIDTH).astype(jnp.float32)
    f = jnp.fft.fft2(a4, axes=(1, 3), norm="ortho").real
    y = jnp.einsum('bsgc,gcd->bsgd', f, w_f.astype(jnp.float32))
    return y.reshape(bn, s, FNET_WIDTH).astype(a.dtype)


def spatial_gating(u, v, ln_g, ln_b, w_s, b_s):
    bn, s, _ = v.shape
    v5 = v.reshape(bn, s // CHUNK, CHUNK, SGU_GROUPS, GROUP_WIDTH)
    v5 = layernorm(v5, ln_g.reshape(SGU_GROUPS, GROUP_WIDTH), ln_b.reshape(SGU_GROUPS, GROUP_WIDTH))
    sv = jnp.einsum('gpq,bnqgc->bnpgc', w_s.astype(v5.dtype), v5)
    sv = sv + b_s.T.astype(v5.dtype)[None, None, :, :, None]
    return u * sv.reshape(bn, s, SGU_WIDTH)


def conformer_conv(z, conv_w, conv_b, ln_g, ln_b):
    bn, s, _ = z.shape
    a, gate = jnp.split(z, 2, axis=-1)
    g = a * jax.nn.sigmoid(gate)
    c = depthwise_conv(g, conv_w, conv_b)
    c = layernorm(c.reshape(bn, s, CONF_GROUPS, GROUP_WIDTH),
                  ln_g.reshape(CONF_GROUPS, GROUP_WIDTH), ln_b.reshape(CONF_GROUPS, GROUP_WIDTH))
    return jax.nn.silu(c.reshape(bn, s, CONF_WIDTH))


def short_gated_conv(bg, cg, xin, w):
    return bg * depthwise_conv(cg * xin, w)


def setup_inputs(seed: int = 0) -> dict:
    key = jax.random.key(seed)
    ks = jax.random.split(key, 32)

    def nrm(k, shape, scale):
        return jax.random.normal(k, shape, jnp.float32) * scale

    def gain(k, shape):
        return 1.0 + 0.05 * jax.random.normal(k, shape, jnp.float32)

    return {
        "x": nrm(ks[0], (BATCH, SEQ, D_MODEL), 1.0),
        "p": nrm(ks[1], (DEPTH, BATCH, SEQ, PLE_DIM), 1.0),
        "mix_pre_g": gain(ks[2], (DEPTH, D_MODEL)),
        "mix_post_g": gain(ks[3], (DEPTH, D_MODEL)),
        "ffn_pre_g": gain(ks[4], (DEPTH, D_MODEL)),
        "ffn_post_g": gain(ks[5], (DEPTH, D_MODEL)),
        "ev_w_in": nrm(ks[6], (N_EVEN, D_MODEL, EVEN_IN), D_MODEL ** -0.5),
        "ev_w_fourier": nrm(ks[7], (N_EVEN, FNET_GROUPS, GROUP_WIDTH, GROUP_WIDTH), GROUP_WIDTH ** -0.5),
        "ev_v_ln_g": gain(ks[8], (N_EVEN, SGU_WIDTH)),
        "ev_v_ln_b": nrm(ks[9], (N_EVEN, SGU_WIDTH), 0.02),
        "ev_w_spatial": nrm(ks[10], (N_EVEN, SGU_GROUPS, CHUNK, CHUNK), CHUNK ** -0.5),
        "ev_b_spatial": 1.0 + nrm(ks[11], (N_EVEN, SGU_GROUPS, CHUNK), 0.1),
        "ev_w_out": nrm(ks[12], (N_EVEN, MIX_WIDTH, D_MODEL), MIX_WIDTH ** -0.5),
        "od_w_in": nrm(ks[13], (N_ODD, D_MODEL, ODD_IN), D_MODEL ** -0.5),
        "od_conv_w": nrm(ks[14], (N_ODD, CONF_KERNEL, CONF_WIDTH), CONF_KERNEL ** -0.5),
        "od_conv_b": nrm(ks[15], (N_ODD, CONF_WIDTH), 0.02),
        "od_ln_g": gain(ks[16], (N_ODD, CONF_WIDTH)),
        "od_ln_b": nrm(ks[17], (N_ODD, CONF_WIDTH), 0.02),
        "od_sconv_w": nrm(ks[18], (N_ODD, SCONV_KERNEL, SCONV_WIDTH), SCONV_KERNEL ** -0.5),
        "od_w_out": nrm(ks[19], (N_ODD, MIX_WIDTH, D_MODEL), MIX_WIDTH ** -0.5),
        "ffn_w_up": nrm(ks[20], (DEPTH, D_MODEL, 2 * D_FF), D_MODEL ** -0.5),
        "ffn_conv_w": nrm(ks[21], (DEPTH, FFN_KERNEL, 2 * D_FF), FFN_KERNEL ** -0.5),
        "ffn_conv_b": nrm(ks[22], (DEPTH, 2 * D_FF), 0.02),
        "ffn_w_down": nrm(ks[23], (DEPTH, D_FF, D_MODEL), D_FF ** -0.5),
        "ple_w_p": nrm(ks[24], (DEPTH, PLE_DIM, D_MODEL), PLE_DIM ** -0.5),
        "ple_gate_g": gain(ks[25], (DEPTH, D_MODEL)),
        "ple_w_g": nrm(ks[26], (DEPTH, D_MODEL, D_MODEL), D_MODEL ** -0.5),
        "ple_b_g": nrm(ks[27], (DEPTH, D_MODEL), 0.02),
    }


def reference(x, p, mix_pre_g, mix_post_g, ffn_pre_g, ffn_post_g,
              ev_w_in, ev_w_fourier, ev_v_ln_g, ev_v_ln_b, ev_w_spatial, ev_b_spatial, ev_w_out,
              od_w_in, od_conv_w, od_conv_b, od_ln_g, od_ln_b, od_sconv_w, od_w_out,
              ffn_w_up, ffn_conv_w, ffn_conv_b, ffn_w_down,
              ple_w_p, ple_gate_g, ple_w_g, ple_b_g):
    for i in range(DEPTH):
        j = i // 2
        h = rmsnorm(x, mix_pre_g[i])
        if i % 2 == 0:
            z = h @ ev_w_in[j]
            za = z[..., :FNET_WIDTH]
            zuv = jax.nn.gelu(z[..., FNET_WIDTH:])
            zu, zv = jnp.split(zuv, 2, axis=-1)
            ya = fourier_mixer(za, ev_w_fourier[j])
            yb = spatial_gating(zu, zv, ev_v_ln_g[j], ev_v_ln_b[j], ev_w_spatial[j], ev_b_spatial[j])
            y = jnp.concatenate([ya, yb], axis=-1) @ ev_w_out[j]
        else:
            z = h @ od_w_in[j]
            zc = z[..., :2 * CONF_WIDTH]
            bg, cg, xin = jnp.split(z[..., 2 * CONF_WIDTH:], 3, axis=-1)
            yc = conformer_conv(zc, od_conv_w[j], od_conv_b[j], od_ln_g[j], od_ln_b[j])
            yd = short_gated_conv(bg, cg, xin, od_sconv_w[j])
            y = jnp.concatenate([yc, yd], axis=-1) @ od_w_out[j]
        x = x + rmsnorm(y, mix_post_g[i])
        h = rmsnorm(x, ffn_pre_g[i])
        up = depthwise_conv(h @ ffn_w_up[i], ffn_conv_w[i], ffn_conv_b[i])
        g, u = jnp.split(up, 2, axis=-1)
        f = (jax.nn.gelu(g) * u) @ ffn_w_down[i]
        x = x + rmsnorm(f, ffn_post_g[i])
        gate = jax.nn.sigmoid(rmsnorm(x, ple_gate_g[i]) @ ple_w_g[i] + ple_b_g[i])
        x = x + gate * (p[i] @ ple_w_p[i])
    return x
```

```python
import numpy as np
import ml_dtypes
import concourse.bass as bass
import concourse.mybir as mybir
from concourse.bass_utils import run_bass_kernel_spmd
from contextlib import ExitStack

F32 = mybir.dt.float32
BF16 = mybir.dt.bfloat16
AF = mybir.ActivationFunctionType
ALU = mybir.AluOpType

D = 1024
S = 4096
DEPTH = 4
DFF = 2816
NFC = 22
PLE = 256
EPS = 1e-6
NCORES = 8

COMPUTE = ("pe", "act", "dve", "pool")
ENGS = ("pe", "act", "dve", "pool", "sp")
N_DMA_SEMS = 32


class T:
    __slots__ = ("name", "t", "w", "r")

    def __init__(self, name, t=None):
        self.name = name
        self.t = t
        self.w = []
        self.r = []

    def __getitem__(self, idx):
        return self.t[idx]


class Rot:
    def __init__(self, items):
        self.items = items
        self.i = 0

    def next(self):
        it = self.items[self.i % len(self.items)]
        self.i += 1
        return it


class Prog:
    def __init__(self, nc):
        self.nc = nc
        self.streams = {e: [] for e in ENGS}
        self.cnt = {e: 0 for e in COMPUTE}
        self.waited = {e: {} for e in ENGS}
        self.dma_i = 0
        self.es = ExitStack()

    def sbuf(self, name, shape, dtype):
        t = self.es.enter_context(self.nc.sbuf_tensor(name, list(shape), dtype))
        return T(name, t)

    def psum(self, name, shape, dtype=F32):
        t = self.es.enter_context(self.nc.psum_tensor(name, list(shape), dtype))
        return T(name, t)

    def dram(self, name, shape, dtype, kind="Internal"):
        t = self.nc.dram_tensor(name, list(shape), dtype, kind=kind)
        return T(name, t.ap())

    def _need(self, eng, ev, raw):
        semkey, val, src = ev
        if src == eng:
            if eng == "pe":
                return
            if not raw:
                return
        if self.waited[eng].get(semkey, 0) >= val:
            return
        self.waited[eng][semkey] = val
        self.streams[eng].append(("wait", semkey, val))

    def _deps(self, eng, reads, writes, appends):
        for t in reads:
            for ev in t.w:
                self._need(eng, ev, True)
        for t in writes:
            for ev in t.w:
                self._need(eng, ev, False)
            for ev in t.r:
                self._need(eng, ev, False)
        for t in appends:
            for ev in t.r:
                self._need(eng, ev, False)

    def _commit(self, ev, reads, writes, appends):
        semkey = ev[0]
        for t in reads:
            if not isinstance(semkey, tuple):
                t.r = [e for e in t.r if e[0] != semkey]
            t.r.append(ev)
        for t in writes:
            t.w = [ev]
            t.r = []
        for t in appends:
            if not isinstance(semkey, tuple):
                t.w = [e for e in t.w if e[0] != semkey]
            t.w.append(ev)
            t.r = []

    def op(self, eng, fn, reads=(), writes=(), appends=()):
        self._deps(eng, reads, writes, appends)
        self.cnt[eng] += 1
        ev = (eng, self.cnt[eng], eng)
        self.streams[eng].append(("op", fn, eng, 1))
        self._commit(ev, reads, writes, appends)
        return ev

    def mm(self, fns, reads=(), writes=(), appends=()):
        eng = "pe"
        self._deps(eng, reads, writes, appends)
        self.cnt[eng] += 1
        ev = (eng, self.cnt[eng], eng)
        for fn in fns[:-1]:
            self.streams[eng].append(("op", fn, None, 0))
        self.streams[eng].append(("op", fns[-1], eng, 1))
        self._commit(ev, reads, writes, appends)
        return ev

    def dma(self, q, fn, reads=(), writes=(), appends=()):
        self._deps(q, reads, writes, appends)
        i = self.dma_i
        self.dma_i += 1
        slot = i % N_DMA_SEMS
        k = i // N_DMA_SEMS
        semkey = ("dma", slot)
        if k > 0:
            self._need(q, (semkey, 16 * k, "dmaq"), True)
        ev = (semkey, 16 * (k + 1), "dmaq")
        self.streams[q].append(("op", fn, semkey, 16))
        self._commit(ev, reads, writes, appends)
        return ev

    def wait_all(self, eng, tiles):
        for t in tiles:
            for ev in t.w:
                self._need(eng, ev, True)

    def emit(self):
        nc = self.nc
        sems = {}
        for e in COMPUTE:
            sems[e] = self.es.enter_context(nc.semaphore("s_" + e))
        for s in range(N_DMA_SEMS):
            sems[("dma", s)] = self.es.enter_context(nc.semaphore("s_dma%d" % s))
        streams = self.streams

        def run(engobj, items):
            for it in items:
                if it[0] == "wait":
                    engobj.wait_ge(sems[it[1]], it[2])
                else:
                    _, fn, semkey, inc = it
                    ins = fn(engobj)
                    if inc:
                        ins.then_inc(sems[semkey], inc)

        with nc.Block() as block:
            @block.tensor
            def _(e):
                run(e, streams["pe"])

            @block.scalar
            def _(e):
                run(e, streams["act"])

            @block.vector
            def _(e):
                run(e, streams["dve"])

            @block.gpsimd
            def _(e):
                run(e, streams["pool"])

            @block.sync
            def _(e):
                run(e, streams["sp"])
        self.es.close()


VEC = {}
_off = 0
for _n, _k in [("mix_pre_g", 8), ("mix_post_g", 8), ("ffn_pre_g", 8), ("ffn_post_g", 8),
               ("ple_gate_g", 8), ("ple_b_g", 8), ("ffn_cw0", 44), ("ffn_cw1", 44),
               ("ffn_cw2", 44), ("ffn_cb", 44), ("ln_g", 4), ("ln_b", 4), ("conv_b", 4),
               ("sconv_w0", 4), ("sconv_w1", 4), ("sconv_w2", 4), ("conv_w", 124)]:
    VEC[_n] = _off
    _off += _k
NV = _off


def _pvec(v):
    return np.ascontiguousarray(np.asarray(v, np.float32).reshape(-1, 128).T)


def _chunk_layout(w):
    k, n = w.shape
    kc, oc = k // 128, n // 128
    return np.ascontiguousarray(
        np.asarray(w, np.float32).reshape(kc, 128, oc, 128).transpose(1, 2, 0, 3)).reshape(128, oc * kc * 128)


def weight_specs():
    specs = []
    for i in range(DEPTH):
        if i % 2 == 0:
            specs += [(f"wau{i}", 8 * 8 * 128), (f"wv{i}", 8 * 512), (f"wf{i}", 512), (f"wst{i}", 512),
                      (f"wo{i}", 8 * 8 * 128)]
        else:
            specs += [(f"win{i}", 20 * 8 * 128), (f"wo{i}", 8 * 8 * 128)]
        specs += [(f"wup{i}", 44 * 8 * 128), (f"wdn{i}", 8 * 22 * 128), (f"wg{i}", 8 * 8 * 128),
                  (f"wp{i}", 8 * 2 * 128)]
    return specs


class Ctx:
    pass


DEBUG_TILES = None
DEBUG_FFN = None


def tiles_for(halo, valid):
    out = []
    vs = 0
    while vs < S:
        wv = min(valid, S - vs)
        out.append((vs - halo, wv + 2 * halo, vs, wv))
        vs += wv
    if len(out) >= 2 and out[-1][3] < valid // 2:
        tot = out[-2][3] + out[-1][3]
        a = (tot // 2 + 1) // 2 * 2
        v0 = out[-2][2]
        out[-2:] = [(v0 - halo, a + 2 * halo, v0, a), (v0 + a - halo, tot - a + 2 * halo, v0 + a, tot - a)]
    if DEBUG_TILES is not None:
        out = [out[i] for i in DEBUG_TILES]
    return out


def xblocks(xb, a, b):
    a = max(a, 0)
    b = min(b, S)
    return [xb[i] for i in range(a // 512, (b - 1) // 512 + 1)]


class HT:
    def __init__(self, ap, deps):
        self.t = ap
        self.deps = deps

    def dep(self, c):
        return [self.deps[c]]

    def all(self):
        return list(dict.fromkeys(self.deps))


def load_x(P, C, src, s, W):
    xw = C.xw.next()
    a, b = max(s, 0), min(s + W, S)
    partial = (a > s or b < s + W)
    if a > s:
        P.op("pool", lambda e: e.memset(xw.t[:, :, 0:a - s], 0.0), writes=[xw])
    if b < s + W:
        P.op("pool", lambda e: e.memset(xw.t[:, :, b - s:W], 0.0), writes=[xw])
    sap = src["ap"].rearrange("(c p) t -> p c t", p=128)
    P.dma("sp", lambda e: e.dma_start(out=xw.t[:, :, a - s:b - s], in_=sap[:, :, a:b]),
          reads=xblocks(src["blk"], a, b), appends=[xw] if partial else (), writes=[] if partial else [xw])
    return xw


def store_x(P, C, dst, xw, off, vs, Wv):
    C.pending.append(lambda: _store_x(P, C, dst, xw, off, vs, Wv))


def _store_x(P, C, dst, xw, off, vs, Wv):
    dap = dst["ap"].rearrange("(c p) t -> p c t", p=128)
    P.dma("sp", lambda e: e.dma_start(out=dap[:, :, vs:vs + Wv], in_=xw.t[:, :, off:off + Wv]),
          reads=[xw], appends=xblocks(dst["blk"], vs, vs + Wv))


def pre_gen(P, C, job, src, s, W, gcol, prep=None):
    if prep is not None:
        prep()
    hT = job["hT"]
    xw = load_x(P, C, src, s, W)
    job["xw"] = xw
    st = C.ps_stat.next()

    def squares(c0):
        sqs = []
        for c in range(c0, c0 + 4):
            sq = C.sq.next()
            P.op("act", lambda e, c=c, sq=sq: e.activation(out=sq.t[:, :W], in_=xw.t[:, c, :W], func=AF.Square),
                 reads=[xw], writes=[sq])
            sqs.append(sq)
        return sqs

    def mms(c0, sqs):
        for i, sq in enumerate(sqs):
            c = c0 + i
            P.mm([lambda e, c=c, sq=sq: e.matmul(st.t[:, :W], lhsT=C.ones.t[:, :], rhs=sq.t[:, :W],
                                                 start=(c == 0), stop=(c == 7))],
                 reads=[sq, C.ones], writes=[st])
    s0 = squares(0)
    yield "A"
    mms(0, s0)
    yield "A"
    s1 = squares(4)
    yield "A"
    mms(4, s1)
    yield "B"
    rs = C.rs.next()
    P.op("act", lambda e: e.activation(out=rs.t[:, :W], in_=st.t[:, :W], func=AF.Sqrt, scale=1.0 / D, bias=C.epsb.t[:, 0:1]),
         reads=[st, C.epsb], writes=[rs])
    P.op("dve", lambda e: e.reciprocal(out=rs.t[:, :W], in_=rs.t[:, :W]), reads=[rs], writes=[rs])
    yield "B"
    for c in range(8):
        P.op("dve", lambda e, c=c: e.scalar_tensor_tensor(
            out=hT.t[:, c, :W], in0=xw.t[:, c, :W], scalar=C.vec.t[:, gcol + c:gcol + c + 1], in1=rs.t[:, :W],
            op0=ALU.mult, op1=ALU.mult), reads=[xw, rs, C.vec], writes=hT.dep(c))
    yield "E"


def wload(P, C, wb, col0, ncols):
    sl = C.wslots.next()
    first = True
    for a in range(0, ncols, 2048):
        b = min(ncols, a + 2048)
        P.dma("sp", lambda e, a=a, b=b: e.dma_start(out=sl.t[:, a:b], in_=wb.t[:, col0 + a:col0 + b]),
              reads=[wb], writes=[sl] if first else (), appends=() if first else [sl])
        first = False
    return sl


class PostNorm:
    def __init__(self, P, C, xw, off, Wv, gcol):
        self.P, self.C, self.xw, self.off, self.Wv, self.gcol = P, C, xw, off, Wv, gcol
        self.st = C.ps_stat.next()
        self.pend = None

    def _flush(self):
        if self.pend is not None:
            sq, first, last = self.pend
            st, Wv, C = self.st, self.Wv, self.C
            self.P.mm([lambda e: e.matmul(st.t[:, :Wv], lhsT=C.ones.t[:, :], rhs=sq.t[:, :Wv], start=first, stop=last)],
                      reads=[sq, C.ones], writes=[st])
            self.pend = None

    def chunk(self, dc, yb):
        P, C, Wv = self.P, self.C, self.Wv
        self._flush()
        sq = C.sq.next()
        P.op("act", lambda e: e.activation(out=sq.t[:, :Wv], in_=yb.t[:, :Wv], func=AF.Square), reads=[yb], writes=[sq])
        yg = C.yg[dc]
        g = self.gcol + dc
        P.op("act", lambda e: e.activation(out=yg.t[:, :Wv], in_=yb.t[:, :Wv], func=AF.Identity, scale=C.vec.t[:, g:g + 1]),
             reads=[yb, C.vec], writes=[yg])
        self.pend = (sq, dc == 0, dc == 7)

    def finish(self):
        P, C, Wv, xw, off = self.P, self.C, self.Wv, self.xw, self.off
        self._flush()
        st = self.st
        rs = C.rs.next()
        P.op("act", lambda e: e.activation(out=rs.t[:, :Wv], in_=st.t[:, :Wv], func=AF.Sqrt, scale=1.0 / D, bias=C.epsb.t[:, 0:1]),
             reads=[st, C.epsb], writes=[rs])
        P.op("dve", lambda e: e.reciprocal(out=rs.t[:, :Wv], in_=rs.t[:, :Wv]), reads=[rs], writes=[rs])
        for dc in range(8):
            yg = C.yg[dc]
            eng = "pool" if dc % 2 == 0 else "dve"
            P.op(eng, lambda e, yg=yg: e.tensor_tensor(out=yg.t[:, :Wv], in0=yg.t[:, :Wv], in1=rs.t[:, :Wv], op=ALU.mult),
                 reads=[rs, yg], writes=[yg])
            P.op("pool", lambda e, yg=yg, dc=dc: e.tensor_tensor(out=xw.t[:, dc, off:off + Wv], in0=xw.t[:, dc, off:off + Wv],
                                                                 in1=yg.t[:, :Wv], op=ALU.add),
                 reads=[yg], writes=[xw])


def out_proj(P, C, wo, mix, xw, off, vs, Wv, gcol, dst):
    pn = PostNorm(P, C, xw, off, Wv, gcol)
    for dc in range(8):
        wsl = wload(P, C, wo, dc * 1024, 1024)
        yb = C.ps.next()
        P.mm([lambda e, kc=kc, wsl=wsl, yb=yb: e.matmul(yb.t[:, :Wv], lhsT=wsl.t[:, kc * 128:(kc + 1) * 128], rhs=mix[kc].t[:, :Wv],
                                                       start=(kc == 0), stop=(kc == 7)) for kc in range(8)],
             reads=[wsl] + list(mix), writes=[yb])
        pn.chunk(dc, yb)
    pn.finish()
    store_x(P, C, dst, xw, off, vs, Wv)


def proj_fm(P, C, wb, oc, hT, W):
    wsl = wload(P, C, wb, oc * 1024, 1024)
    bk = C.ps.next()
    P.mm([lambda e, kc=kc: e.matmul(bk.t[:, :W], lhsT=wsl.t[:, kc * 128:(kc + 1) * 128], rhs=hT.t[:, kc, :W],
                                    start=(kc == 0), stop=(kc == 7)) for kc in range(8)],
         reads=[wsl] + hT.all(), writes=[bk])
    return bk


def sub_ple(P, C, li, src, dst):
    vb = li * NV
    wg, wp = C.wb[f"wg{li}"], C.wb[f"wp{li}"]
    jobs = []
    WG0, WP0 = 0, 16 * 512
    wg_blks = [C.big[i] for i in range(16)]
    wp_blks = [C.big[16 + i] for i in range(4)]

    def load_weights():
        for dc in range(8):
            for h in range(2):
                P.dma("sp", lambda e, dc=dc, h=h: e.dma_start(out=C.bigt[:, WG0 + dc * 1024 + h * 512:WG0 + dc * 1024 + (h + 1) * 512],
                                                             in_=wg.t[:, dc * 1024 + h * 512:dc * 1024 + (h + 1) * 512]),
                      reads=[wg], writes=[wg_blks[2 * dc + h]])
        for q in range(4):
            P.dma("sp", lambda e, q=q: e.dma_start(out=C.bigt[:, WP0 + q * 512:WP0 + (q + 1) * 512], in_=wp.t[:, q * 512:(q + 1) * 512]),
                  reads=[wp], writes=[wp_blks[q]])

    def mk(first, s, W, vs, Wv):
        def pre(job):
            def prep():
                pb = C.pb.next()
                job["pb"] = pb
                P.dma("sp", lambda e: e.dma_start(out=pb.t[:, :, :W], in_=C.pTb.t[li].rearrange("(c p) t -> p c t", p=128)[:, :, s:s + W]),
                      reads=[C.pTb_dep[li]], writes=[pb])
            return pre_gen(P, C, job, src, s, W, vb + VEC["ple_gate_g"], prep=prep)

        def body(job, inject):
            xw, hT, pb = job["xw"], job["hT"], job["pb"]
            if first:
                load_weights()
            for dc in range(8):
                gb = C.ps.next()
                P.mm([lambda e, kc=kc, gb=gb, dc=dc: e.matmul(
                    gb.t[:, :W], lhsT=C.bigt[:, WG0 + dc * 1024 + kc * 128:WG0 + dc * 1024 + (kc + 1) * 128], rhs=hT.t[:, kc, :W],
                    start=(kc == 0), stop=(kc == 7)) for kc in range(8)],
                    reads=[wg_blks[2 * dc], wg_blks[2 * dc + 1]] + hT.all(), writes=[gb])
                ppb = C.ps.next()
                P.mm([lambda e, kc=kc, ppb=ppb, dc=dc: e.matmul(
                    ppb.t[:, :W], lhsT=C.bigt[:, WP0 + dc * 256 + kc * 128:WP0 + dc * 256 + (kc + 1) * 128], rhs=pb.t[:, kc, :W],
                    start=(kc == 0), stop=(kc == 1)) for kc in range(2)],
                    reads=[wp_blks[dc // 2], pb], writes=[ppb])
                gt = C.f32.next()
                bcol = vb + VEC["ple_b_g"] + dc
                P.op("act", lambda e, gt=gt, gb=gb, bcol=bcol: e.activation(out=gt.t[:, :W], in_=gb.t[:, :W], func=AF.Sigmoid,
                                                                          bias=C.vec.t[:, bcol:bcol + 1]),
                     reads=[gb, C.vec], writes=[gt])
                P.op("dve", lambda e, gt=gt, ppb=ppb: e.tensor_tensor(out=gt.t[:, :W], in0=gt.t[:, :W], in1=ppb.t[:, :W], op=ALU.mult),
                     reads=[ppb, gt], writes=[gt])
                P.op("pool", lambda e, gt=gt, dc=dc: e.tensor_tensor(out=xw.t[:, dc, :W], in0=xw.t[:, dc, :W], in1=gt.t[:, :W], op=ALU.add),
                     reads=[gt], writes=[xw])
                if dc in (1, 2, 4):
                    inject(2)
            store_x(P, C, dst, xw, 0, vs, Wv)
        return {"kind": "ple", "pre": pre, "body": body}
    for i, tl_ in enumerate(tiles_for(0, 512)):
        jobs.append(mk(i == 0, *tl_))
    return jobs


def sub_ffn(P, C, li, src, dst):
    vb = li * NV
    wup, wdn = C.wb[f"wup{li}"], C.wb[f"wdn{li}"]
    jobs = []

    def mk(s, W, vs, Wv):
        def pre(job):
            return pre_gen(P, C, job, src, s, W, vb + VEC["ffn_pre_g"])

        def body(job, inject):
            xw, hT = job["xw"], job["hT"]
            acts = []
            pend2 = [None]
            for fc in range(NFC):
                wsl = wload(P, C, wup, fc * 2048, 2048)
                banks = []
                for h in range(2):
                    bk = C.ps.next()
                    P.mm([lambda e, kc=kc, h=h, wsl=wsl, bk=bk: e.matmul(
                        bk.t[:, :W], lhsT=wsl.t[:, h * 1024 + kc * 128:h * 1024 + (kc + 1) * 128], rhs=hT.t[:, kc, :W],
                        start=(kc == 0), stop=(kc == 7)) for kc in range(8)], reads=[wsl] + hT.all(), writes=[bk])
                    banks.append(bk)
                tmps = [C.f32.next(), C.f32.next()]
                cols = []
                for h in range(2):
                    ch = fc + h * NFC
                    cols.append((vb + VEC["ffn_cw0"] + ch, vb + VEC["ffn_cw1"] + ch, vb + VEC["ffn_cw2"] + ch, vb + VEC["ffn_cb"] + ch))
                for h in range(2):
                    bk, tt = banks[h], tmps[h]
                    c0, c1, c2, cb = cols[h]
                    P.op("act", lambda e, tt=tt, bk=bk, c1=c1, cb=cb: e.activation(
                        out=tt.t[:, :Wv], in_=bk.t[:, 1:1 + Wv], func=AF.Identity, scale=C.vec.t[:, c1:c1 + 1],
                        bias=C.vec.t[:, cb:cb + 1]), reads=[bk, C.vec], writes=[tt])
                for tap in (0, 2):
                    for h in range(2):
                        bk, tt = banks[h], tmps[h]
                        cc_ = cols[h][tap]
                        P.op("dve", lambda e, tt=tt, bk=bk, cc_=cc_, tap=tap: e.scalar_tensor_tensor(
                            out=tt.t[:, :Wv], in0=bk.t[:, tap:tap + Wv], scalar=C.vec.t[:, cc_:cc_ + 1], in1=tt.t[:, :Wv],
                            op0=ALU.mult, op1=ALU.add), reads=[bk, C.vec, tt], writes=[tt])
                def stage2(fc=fc, tmps=tmps):
                    P.op("act", lambda e, tt=tmps[0]: e.activation(out=tt.t[:, :Wv], in_=tt.t[:, :Wv], func=AF.Gelu),
                         reads=[tmps[0]], writes=[tmps[0]])
                    ab = C.big[fc]
                    P.op("pool", lambda e, fc=fc, a=tmps[0], b=tmps[1]: e.tensor_tensor(
                        out=C.bigt[:, fc * 512:fc * 512 + Wv], in0=a.t[:, :Wv], in1=b.t[:, :Wv], op=ALU.mult),
                        reads=[tmps[0], tmps[1]], writes=[ab])
                    acts.append(ab)
                if pend2[0] is not None:
                    pend2[0]()
                pend2[0] = stage2
                if fc == 2:
                    inject(0)
                elif fc in (12, 14, 15, 17):
                    inject(1)
            pend2[0]()
            pn = PostNorm(P, C, xw, 1, Wv, vb + VEC["ffn_post_g"])
            for dc in range(8):
                HF = NFC // 2
                wh = [wload(P, C, wdn, dc * NFC * 128, HF * 128), wload(P, C, wdn, dc * NFC * 128 + HF * 128, HF * 128)]
                yb = C.ps.next()
                P.mm([lambda e, fc=fc, wh=wh, yb=yb: e.matmul(
                    yb.t[:, :Wv], lhsT=wh[fc // HF].t[:, (fc % HF) * 128:(fc % HF + 1) * 128], rhs=C.bigt[:, fc * 512:fc * 512 + Wv],
                    start=(fc == 0), stop=(fc == NFC - 1)) for fc in range(NFC)], reads=wh + acts, writes=[yb])
                pn.chunk(dc, yb)
                if dc == 0:
                    inject(2)
            pn.finish()
            store_x(P, C, dst, xw, 1, vs, Wv)
        return {"kind": "ffn", "pre": pre, "body": body}
    for tl_ in tiles_for(1, 510):
        jobs.append(mk(*tl_))
    return jobs


def prep_odd(P, C, li):
    vb = li * NV
    for k in range(31):
        for g in range(4):
            idx = k * 4 + g
            col = vb + VEC["conv_w"] + idx
            blk = C.big[32 + idx // 4]
            P.op("pool", lambda e, idx=idx, col=col: e.tensor_scalar(
                out=C.bigt[:, 16384 + idx * 128:16384 + (idx + 1) * 128], in0=C.ident.t[:, :],
                scalar1=C.vec.t[:, col:col + 1], scalar2=0.0, op0=ALU.mult, op1=ALU.add),
                reads=[C.ident, C.vec], appends=[blk])


def sub_odd(P, C, li, src, dst):
    vb = li * NV
    win, wo = C.wb[f"win{li}"], C.wb[f"wo{li}"]
    diag_blks = [C.big[32 + i] for i in range(31)]
    H = 16
    jobs = []

    def mk(first, s, W, vs, Wv):
        def pre(job):
            return pre_gen(P, C, job, src, s, W, vb + VEC["mix_pre_g"], prep=(lambda: prep_odd(P, C, li)) if first else None)

        def body(job, inject):
            xw, hT = job["xw"], job["hT"]
            mix = [None] * 8
            st = [dict() for _ in range(4)]

            def S1(g):
                d = st[g]
                ab = proj_fm(P, C, win, g, hT, W)
                gb = proj_fm(P, C, win, 4 + g, hT, W)
                sg = C.f32.next()
                P.op("act", lambda e: e.activation(out=sg.t[:, :W], in_=gb.t[:, :W], func=AF.Sigmoid), reads=[gb], writes=[sg])
                gl = C.b16.next()
                P.op("dve", lambda e: e.tensor_tensor(out=gl.t[:, :W], in0=ab.t[:, :W], in1=sg.t[:, :W], op=ALU.mult),
                     reads=[ab, sg], writes=[gl])
                d.update(sg=sg, gl=gl)

            def S2(g):
                d = st[g]
                cgb = proj_fm(P, C, win, 12 + g, hT, W)
                xb_ = proj_fm(P, C, win, 16 + g, hT, W)
                xs = C.f32.next()
                P.op("act", lambda e: e.activation(out=xs.t[:, :W], in_=xb_.t[:, :W], func=AF.Copy), reads=[xb_], writes=[xs])
                P.op("dve", lambda e: e.tensor_tensor(out=xs.t[:, :W], in0=cgb.t[:, :W], in1=xs.t[:, :W], op=ALU.mult),
                     reads=[cgb, xs], writes=[xs])
                d.update(xs=xs)

            def S3(g):
                d = st[g]
                gl = d["gl"]
                cbk = C.ps.next()
                P.mm([lambda e, k=k: e.matmul(
                    cbk.t[:, :Wv], lhsT=C.bigt[:, 16384 + (k * 4 + g) * 128:16384 + (k * 4 + g + 1) * 128],
                    rhs=gl.t[:, H - 15 + k:H - 15 + k + Wv], start=(k == 0), stop=(k == 30)) for k in range(31)],
                    reads=[gl] + diag_blks, writes=[cbk])
                if g >= 1:
                    inject(2)
                cb = C.f32.next()
                bcol = vb + VEC["conv_b"] + g
                P.op("act", lambda e: e.activation(out=cb.t[:, :Wv], in_=cbk.t[:, :Wv], func=AF.Identity, bias=C.vec.t[:, bcol:bcol + 1]),
                     reads=[cbk, C.vec], writes=[cb])
                c16 = C.b16.next()
                P.op("act", lambda e: e.activation(out=c16.t[:, :Wv], in_=cbk.t[:, :Wv], func=AF.Identity, bias=C.vec.t[:, bcol:bcol + 1]),
                     reads=[cbk, C.vec], writes=[c16])
                d.update(cb=cb, c16=c16)

            def S4(g):
                d = st[g]
                cb, c16, xs = d["cb"], d["c16"], d["xs"]
                mb = C.ps.next()
                P.mm([lambda e: e.matmul(mb.t[:, :Wv], lhsT=C.onesdiv.t[:, :], rhs=c16.t[:, :Wv], start=True, stop=True)],
                     reads=[c16, C.onesdiv], writes=[mb])
                bb = proj_fm(P, C, win, 8 + g, hT, W)
                P.op("dve", lambda e: e.tensor_tensor(out=cb.t[:, :Wv], in0=cb.t[:, :Wv], in1=mb.t[:, :Wv], op=ALU.subtract),
                     reads=[mb, cb], writes=[cb])
                dsq = C.b16.next()
                P.op("act", lambda e: e.activation(out=dsq.t[:, :Wv], in_=cb.t[:, :Wv], func=AF.Square), reads=[cb], writes=[dsq])
                tt = C.f32.next()
                w0 = vb + VEC["sconv_w0"] + g
                P.op("dve", lambda e: e.tensor_scalar(out=tt.t[:, :Wv], in0=xs.t[:, H - 1:H - 1 + Wv],
                                                      scalar1=C.vec.t[:, w0:w0 + 1], scalar2=0.0, op0=ALU.mult, op1=ALU.add),
                     reads=[xs, C.vec], writes=[tt])
                d.update(bb=bb, dsq=dsq, tt=tt)

            def S5(g):
                d = st[g]
                sd, cb, xs, tt, bb, dsq = d["sg"], d["cb"], d["xs"], d["tt"], d["bb"], d["dsq"]
                w1, w2 = vb + VEC["sconv_w1"] + g, vb + VEC["sconv_w2"] + g
                vbk = C.ps.next()
                P.mm([lambda e: e.matmul(vbk.t[:, :Wv], lhsT=C.onesdiv.t[:, :], rhs=dsq.t[:, :Wv], start=True, stop=True)],
                     reads=[dsq, C.onesdiv], writes=[vbk])
                P.op("act", lambda e: e.activation(out=sd.t[:, :Wv], in_=vbk.t[:, :Wv], func=AF.Sqrt, bias=C.epsb.t[:, 0:1]),
                     reads=[vbk, C.epsb], writes=[sd])
                P.op("dve", lambda e: e.scalar_tensor_tensor(
                    out=tt.t[:, :Wv], in0=xs.t[:, H:H + Wv], scalar=C.vec.t[:, w1:w1 + 1], in1=tt.t[:, :Wv],
                    op0=ALU.mult, op1=ALU.add), reads=[xs, C.vec, tt], writes=[tt])
                P.op("dve", lambda e: e.reciprocal(out=sd.t[:, :Wv], in_=sd.t[:, :Wv]), reads=[sd], writes=[sd])
                P.op("dve", lambda e: e.scalar_tensor_tensor(
                    out=tt.t[:, :Wv], in0=xs.t[:, H + 1:H + 1 + Wv], scalar=C.vec.t[:, w2:w2 + 1], in1=tt.t[:, :Wv],
                    op0=ALU.mult, op1=ALU.add), reads=[xs, C.vec, tt], writes=[tt])
                P.op("dve", lambda e: e.tensor_tensor(out=cb.t[:, :Wv], in0=cb.t[:, :Wv], in1=sd.t[:, :Wv], op=ALU.mult),
                     reads=[sd, cb], writes=[cb])
                mx = C.mix[g]
                gcol, bcol2 = vb + VEC["ln_g"] + g, vb + VEC["ln_b"] + g
                P.op("act", lambda e: e.activation(out=mx.t[:, :Wv], in_=cb.t[:, :Wv], func=AF.Silu, scale=C.vec.t[:, gcol:gcol + 1],
                                                   bias=C.vec.t[:, bcol2:bcol2 + 1]), reads=[cb, C.vec], writes=[mx])
                mix[g] = mx
                mx2 = C.mix[4 + g]
                P.op("dve", lambda e: e.tensor_tensor(out=mx2.t[:, :Wv], in0=bb.t[:, H:H + Wv], in1=tt.t[:, :Wv], op=ALU.mult),
                     reads=[bb, tt], writes=[mx2])
                mix[4 + g] = mx2

            S1(0)
            S2(0)
            S3(0)
            for g in range(1, 4):
                S1(g)
                S4(g - 1)
                S2(g)
                S5(g - 1)
                S3(g)
            S4(3)
            S5(3)
            out_proj(P, C, wo, mix, xw, H, vs, Wv, vb + VEC["mix_post_g"], dst)
        return {"kind": "odd", "pre": pre, "body": body}
    for i, tl_ in enumerate(tiles_for(H, 480)):
        jobs.append(mk(i == 0, *tl_))
    return jobs


def prep_even(P, C, li):
    vb = li * NV
    wv, wf, wst = C.wb[f"wv{li}"], C.wb[f"wf{li}"], C.wb[f"wst{li}"]
    bs = C.bs[li]
    wfs = wload(P, C, wf, 0, 512)
    wcs = C.wcs
    for g in range(4):
        bk = C.ps.next()
        P.mm([lambda e, g=g, bk=bk: e.matmul(bk.t[:, 0:128], lhsT=C.cc.t[:, 0:128], rhs=wfs.t[:, g * 128:(g + 1) * 128],
                                             start=True, stop=True),
              lambda e, g=g, bk=bk: e.matmul(bk.t[:, 128:256], lhsT=C.cc.t[:, 128:256], rhs=wfs.t[:, g * 128:(g + 1) * 128],
                                             start=True, stop=True)], reads=[wfs, C.cc], writes=[bk])
        P.op("act", lambda e, g=g, bk=bk: e.activation(out=wcs.t[:, g * 256:(g + 1) * 256], in_=bk.t[:, 0:256], func=AF.Copy),
             reads=[bk], appends=[wcs])
    wsts = C.wsts
    P.dma("sp", lambda e: e.dma_start(out=wsts.t[:, :], in_=wst.t[:, :]), reads=[wst], writes=[wsts])
    wst32 = C.f32.next()
    P.dma("sp", lambda e: e.dma_start(out=wst32.t[:, :], in_=C.win32[f"wst{li}"].t[:, :]), reads=[], writes=[wst32])
    bs0 = C.f32.next()
    P.op("pool", lambda e: e.memset(bs0.t[:, :], 0.0), writes=[bs0])
    P.dma("sp", lambda e: e.dma_start(out=bs0.t[0:1, :], in_=bs.t[0:1, :]), reads=[], appends=[bs0])
    bias2 = C.bias2
    for g in range(4):
        bk = C.ps.next()
        P.mm([lambda e, g=g, bk=bk: e.matmul(bk.t[:, 0:128], lhsT=C.ones32.t[:, :], rhs=wst32.t[:, g * 128:(g + 1) * 128],
                                             start=True, stop=True),
              lambda e, g=g, bk=bk: e.matmul(bk.t[:, 128:256], lhsT=C.ones32.t[:, :], rhs=bs0.t[:, g * 128:(g + 1) * 128],
                                             start=True, stop=True)], reads=[wst32, bs0, C.ones32], writes=[bk])
        lb = vb + VEC["ln_b"] + g
        tmp = C.f32.next()
        P.op("act", lambda e, bk=bk, tmp=tmp: e.activation(out=tmp.t[:, 0:128], in_=bk.t[:, 128:256], func=AF.Copy),
             reads=[bk], writes=[tmp])
        P.op("dve", lambda e, g=g, bk=bk, lb=lb, tmp=tmp: e.scalar_tensor_tensor(
            out=bias2.t[:, g * 128:(g + 1) * 128], in0=bk.t[:, 0:128], scalar=C.vec.t[:, lb:lb + 1], in1=tmp.t[:, 0:128],
            op0=ALU.mult, op1=ALU.add), reads=[bk, tmp, C.vec], appends=[bias2])


def sub_even(P, C, li, src, dst):
    vb = li * NV
    wau, wo = C.wb[f"wau{li}"], C.wb[f"wo{li}"]
    wcs, wsts, bias2 = C.wcs, C.wsts, C.bias2
    wv = C.wb[f"wv{li}"]
    tl = tiles_for(0, 512)
    jobs = []

    def mk1(ti, s, W, vs, Wv):
        def pre(job):
            return pre_gen(P, C, job, src, s, W, vb + VEC["mix_pre_g"], prep=(lambda: prep_even(P, C, li)) if ti == 0 else None)

        def body(job, inject):
            hT = job["hT"]
            ats = []
            for g in range(4):
                bk = proj_fm(P, C, wau, g, hT, W)
                at = C.b16.next()
                P.op("act", lambda e, at=at, bk=bk: e.activation(out=at.t[:, :W], in_=bk.t[:, :W], func=AF.Copy), reads=[bk], writes=[at])
                ats.append(at)
            inject(2)
            for ts_ in range(4):
                sc = ti * 4 + ts_
                for half in range(2):
                    bk = C.ps.next()
                    P.mm([lambda e, g=g, ts_=ts_, bk=bk, j=j: e.matmul(
                        bk.t[:, j * 256:(j + 1) * 256], lhsT=ats[g].t[:, ts_ * 128:(ts_ + 1) * 128], rhs=wcs.t[:, g * 256:(g + 1) * 256],
                        start=True, stop=True) for j, g in enumerate((2 * half, 2 * half + 1))],
                        reads=ats + [wcs], writes=[bk])
                    blk = C.big[2 * sc + half]
                    if half == 0:
                        P.op("act", lambda e, sc=sc, half=half, bk=bk: e.activation(
                            out=C.bigt[:, sc * 1024 + half * 512:sc * 1024 + (half + 1) * 512], in_=bk.t[:, :], func=AF.Copy),
                            reads=[bk], writes=[blk])
                    else:
                        P.op("dve", lambda e, sc=sc, half=half, bk=bk: e.tensor_copy(
                            out=C.bigt[:, sc * 1024 + half * 512:sc * 1024 + (half + 1) * 512], in_=bk.t[:, :]),
                            reads=[bk], writes=[blk])
                if ts_ in (0, 2):
                    inject(2)
        return {"kind": "even", "pre": pre, "body": body}
    for ti, tl_ in enumerate(tl):
        jobs.append(mk1(ti, *tl_))

    allB = [C.big[i] for i in range(64)]

    def mk2(ti, s, W, vs, Wv):
        def pre(job):
            return pre_gen(P, C, job, src, s, W, vb + VEC["mix_pre_g"])

        def body(job, inject):
            xw, hT = job["xw"], job["hT"]
            vns = []
            wvh = [wload(P, C, wv, 0, 2048), wload(P, C, wv, 2048, 2048)]
            for ts_ in range(4):
                bk = C.ps.next()
                P.mm([lambda e, kc=kc, ts_=ts_, bk=bk: e.matmul(bk.t[:, :], lhsT=hT.t[:, kc, ts_ * 128:(ts_ + 1) * 128],
                                                               rhs=wvh[kc // 4].t[:, (kc % 4) * 512:(kc % 4 + 1) * 512],
                                                               start=(kc == 0), stop=(kc == 7))
                      for kc in range(8)], reads=hT.all() + wvh, writes=[bk])
                vg = C.f32.next()
                P.op("act", lambda e, vg=vg, bk=bk: e.activation(out=vg.t[:, :], in_=bk.t[:, :], func=AF.Gelu), reads=[bk], writes=[vg])
                stt = C.stt.next()
                for g in range(4):
                    P.op("dve", lambda e, g=g, vg=vg, stt=stt: e.bn_stats(out=stt.t[:, g * 6:(g + 1) * 6], in_=vg.t[:, g * 128:(g + 1) * 128]),
                         reads=[vg], appends=[stt])
                mv = C.mv.next()
                for g in range(4):
                    P.op("dve", lambda e, g=g, mv=mv, stt=stt: e.bn_aggr(out=mv.t[:, g * 2:(g + 1) * 2], in_=stt.t[:, g * 6:(g + 1) * 6]),
                         reads=[stt], appends=[mv])
                rr = C.rr.next()
                P.op("act", lambda e, rr=rr, mv=mv: e.activation(out=rr.t[:, 0:4], in_=mv.t[:, 1:8:2], func=AF.Sqrt, bias=C.epsb.t[:, 0:1]),
                     reads=[mv, C.epsb], writes=[rr])
                P.op("dve", lambda e, rr=rr: e.reciprocal(out=rr.t[:, 0:4], in_=rr.t[:, 0:4]), reads=[rr], writes=[rr])
                vn = C.b16.next()
                for g in range(4):
                    P.op("dve", lambda e, g=g, vn=vn, vg=vg, mv=mv, rr=rr: e.tensor_scalar(
                        out=vn.t[:, g * 128:(g + 1) * 128], in0=vg.t[:, g * 128:(g + 1) * 128], scalar1=mv.t[:, 2 * g:2 * g + 1],
                        scalar2=rr.t[:, g:g + 1], op0=ALU.subtract, op1=ALU.mult), reads=[vg, mv, rr], appends=[vn])
                vns.append(vn)
            uts = []
            for g in range(4):
                bk = proj_fm(P, C, wau, 4 + g, hT, W)
                ut = C.ut[g]
                P.op("act", lambda e, ut=ut, bk=bk: e.activation(out=ut.t[:, :W], in_=bk.t[:, :W], func=AF.Gelu), reads=[bk], writes=[ut])
                uts.append(ut)
            inject(2)
            for ts_ in range(4):
                vn = vns[ts_]
                mbk = C.ps.next()
                P.mm([lambda e, g=g, vn=vn, mbk=mbk: e.matmul(mbk.t[:, g * 128:(g + 1) * 128], lhsT=vn.t[:, g * 128:(g + 1) * 128],
                                                             rhs=wsts.t[:, g * 128:(g + 1) * 128], start=True, stop=True) for g in range(4)],
                     reads=[vn, wsts], writes=[mbk])
                for g in range(4):
                    tt = C.f32.next()
                    lg = vb + VEC["ln_g"] + g
                    P.op("dve", lambda e, g=g, tt=tt, mbk=mbk, lg=lg: e.scalar_tensor_tensor(
                        out=tt.t[:, 0:128], in0=mbk.t[:, g * 128:(g + 1) * 128], scalar=C.vec.t[:, lg:lg + 1],
                        in1=bias2.t[:, g * 128:(g + 1) * 128], op0=ALU.mult, op1=ALU.add), reads=[mbk, bias2, C.vec], writes=[tt])
                    mx = C.mix[4 + g]
                    P.op("pool", lambda e, g=g, tt=tt, mx=mx, ts_=ts_: e.tensor_tensor(
                        out=mx.t[:, ts_ * 128:(ts_ + 1) * 128], in0=tt.t[:, 0:128], in1=uts[g].t[:, ts_ * 128:(ts_ + 1) * 128],
                        op=ALU.mult), reads=[tt, uts[g]], appends=[mx])
            fb = [C.ps.next() for _ in range(4)]
            for sb in range(8):
                col0 = ti * 16384 + sb * 2048
                csl = wload(P, C, C.dftc, col0, 2048)
                ssl = wload(P, C, C.dfts, col0, 2048)
                fns = []
                for g in range(4):
                    for q in range(4):
                        sc = sb * 4 + q
                        fns.append(lambda e, g=g, q=q, sc=sc, csl=csl: e.matmul(
                            fb[g].t[:, :], lhsT=C.bigt[:, sc * 1024 + g * 256:sc * 1024 + g * 256 + 128], rhs=csl.t[:, q * 512:(q + 1) * 512],
                            start=(sc == 0), stop=False))
                        fns.append(lambda e, g=g, q=q, sc=sc, ssl=ssl: e.matmul(
                            fb[g].t[:, :], lhsT=C.bigt[:, sc * 1024 + g * 256 + 128:sc * 1024 + g * 256 + 256], rhs=ssl.t[:, q * 512:(q + 1) * 512],
                            start=False, stop=(sc == 31)))
                P.mm(fns, reads=[csl, ssl] + allB, writes=fb)
                if sb in (1, 3):
                    inject(2)
            mix = []
            for g in range(4):
                mx = C.mix[g]
                P.op("act", lambda e, g=g, mx=mx: e.activation(out=mx.t[:, :], in_=fb[g].t[:, :], func=AF.Copy), reads=[fb[g]], writes=[mx])
                mix.append(mx)
            mix += [C.mix[4 + g] for g in range(4)]
            out_proj(P, C, wo, mix, xw, 0, vs, Wv, vb + VEC["mix_post_g"], dst)
        return {"kind": "even", "pre": pre, "body": body}
    for ti, tl_ in enumerate(tl):
        jobs.append(mk2(ti, *tl_))
    return jobs


def build_program(plan, need_dft=True):
    nc = bass.Bass("TRN2", target_bir_lowering=False)
    P = Prog(nc)
    C = Ctx()
    xin = P.dram("xT", [D, S], F32, kind="ExternalInput")
    C.pT = P.dram("pT", [DEPTH, PLE, S], F32, kind="ExternalInput")
    vecs = P.dram("vecs", [128, DEPTH * NV], F32, kind="ExternalInput")
    out = P.dram("outT", [D, S], F32, kind="ExternalOutput")
    layers = sorted(set(l for l, _ in plan))
    kinds = set(plan)
    C.win32, C.wb, C.bs = {}, {}, {}
    used = []

    def wkind(name):
        return "mix" if name[:-1] in ("wau", "wv", "wf", "wst", "wo", "win") else ("ffn" if name[:-1] in ("wup", "wdn") else "ple")
    for name, ncols in weight_specs():
        li = int(name[-1])
        if (li, wkind(name)) not in kinds:
            continue
        C.win32[name] = P.dram(name, [128, ncols], F32, kind="ExternalInput")
        C.wb[name] = P.dram(name + "_b", [128, ncols], BF16)
        used.append((name, ncols))
    for li in layers:
        if li % 2 == 0 and (li, "mix") in kinds:
            C.bs[li] = P.dram(f"bs{li}", [1, 512], F32, kind="ExternalInput")
    has_even = any(l % 2 == 0 and k == "mix" for l, k in plan)
    has_odd = any(l % 2 == 1 and k == "mix" for l, k in plan)
    if has_even:
        C.dftc = P.dram("dftc", [128, 8 * 32 * 512], BF16, kind="ExternalInput")
        C.dfts = P.dram("dfts", [128, 8 * 32 * 512], BF16, kind="ExternalInput")
        ccd = P.dram("cc", [128, 256], BF16, kind="ExternalInput")
    if has_odd:
        identd = P.dram("ident", [128, 128], BF16, kind="ExternalInput")
    xa = P.dram("xa", [D, S], F32)
    xb = P.dram("xb", [D, S], F32)
    C.pTb = P.dram("pTb", [DEPTH, PLE, S], BF16)
    C.pTb_dep = [T("pTb%d" % i) for i in range(DEPTH)]

    def xbuf(t):
        return {"ap": t.t, "blk": [T(t.name + "_b%d" % i) for i in range(8)]}
    bufs = [xbuf(xin)] + [None] * (len(plan) - 1)
    xab = [xbuf(xa), xbuf(xb)]
    for j in range(1, len(plan)):
        bufs[j] = xab[j % 2]
    bufs.append(xbuf(out))

    C.vec = P.sbuf("vec", [128, DEPTH * NV], F32)
    C.ones = P.sbuf("ones", [128, 128], BF16)
    C.onesdiv = P.sbuf("onesdiv", [128, 128], BF16)
    C.epsb = P.sbuf("epsb", [128, 1], F32)
    C.xw = Rot([P.sbuf("xw%d" % i, [128, 8, 512], F32) for i in range(2)])
    hT0 = P.sbuf("hT0", [128, 8, 512], BF16)
    C.sq = Rot([P.sbuf("sq%d" % i, [128, 512], BF16) for i in range(4)])
    C.rs = Rot([P.sbuf("rs%d" % i, [128, 512], F32) for i in range(2)])
    C.yg = [P.sbuf("yg%d" % i, [128, 512], F32) for i in range(8)]
    C.wslots = Rot([P.sbuf("wsl%d" % i, [128, 2048], BF16) for i in range(6)])
    bigt = P.sbuf("big", [128, 32768], BF16)
    C.bigt = bigt.t
    C.big = [T("big%d" % i) for i in range(64)]
    hts = [HT(hT0.t, [hT0] * 8),
           HT(C.bigt[:, 12288:16384].rearrange("p (c w) -> p c w", c=8), [C.big[24 + c] for c in range(8)])]
    C.f32 = Rot([P.sbuf("f32_%d" % i, [128, 512], F32) for i in range(8)])
    C.b16 = Rot([P.sbuf("b16_%d" % i, [128, 512], BF16) for i in range(8)])
    C.mix = [P.sbuf("mix%d" % i, [128, 512], BF16) for i in range(8)]
    C.pb = Rot([P.sbuf("pb%d" % i, [128, 2, 512], BF16) for i in range(2)])
    if has_even:
        C.cc = P.sbuf("cc_s", [128, 256], BF16)
        C.wcs = P.sbuf("wcs", [128, 1024], BF16)
        C.wsts = P.sbuf("wsts", [128, 512], BF16)
        C.bias2 = P.sbuf("bias2", [128, 512], F32)
        C.ones32 = P.sbuf("ones32", [128, 128], F32)
        C.ut = [P.sbuf("ut%d" % i, [128, 512], BF16) for i in range(4)]
        C.stt = Rot([P.sbuf("stt%d" % i, [128, 24], F32) for i in range(4)])
        C.mv = Rot([P.sbuf("mv%d" % i, [128, 8], F32) for i in range(4)])
        C.rr = Rot([P.sbuf("rr%d" % i, [128, 4], F32) for i in range(4)])
    if has_odd:
        C.ident = P.sbuf("ident_s", [128, 128], BF16)
    C.ps = Rot([P.psum("ps%d" % i, [128, 512]) for i in range(6)])
    C.ps_stat = Rot([P.psum("pst%d" % i, [128, 512]) for i in range(2)])

    P.op("pool", lambda e: e.memset(C.ones.t[:, :], 1.0), writes=[C.ones])
    P.op("pool", lambda e: e.memset(C.onesdiv.t[:, :], 1.0 / 128), writes=[C.onesdiv])
    P.op("pool", lambda e: e.memset(C.epsb.t[:, :], EPS), writes=[C.epsb])
    P.dma("sp", lambda e: e.dma_start(out=C.vec.t[:, :], in_=vecs.t[:, :]), writes=[C.vec])
    if has_even:
        P.op("pool", lambda e: e.memset(C.ones32.t[:, :], 1.0), writes=[C.ones32])
        P.dma("sp", lambda e: e.dma_start(out=C.cc.t[:, :], in_=ccd.t[:, :]), writes=[C.cc])
    if has_odd:
        P.dma("sp", lambda e: e.dma_start(out=C.ident.t[:, :], in_=identd.t[:, :]), writes=[C.ident])

    def casts_for(k):
        li, kind = plan[k]
        todo = []
        for name, ncols in used:
            if int(name[-1]) == li and wkind(name) == kind:
                if name.startswith("wg"):
                    for h in range(2):
                        todo.append(lambda li=li, h=h: P.dma(
                            "pool", lambda e: e.dma_start(out=C.pTb.t[li, :, h * 2048:(h + 1) * 2048], in_=C.pT.t[li, :, h * 2048:(h + 1) * 2048]),
                            appends=[C.pTb_dep[li]]))
                step = 8192
                for c0 in range(0, ncols, step):
                    c1 = min(ncols, c0 + step)
                    todo.append(lambda name=name, c0=c0, c1=c1: P.dma(
                        "pool", lambda e: e.dma_start(out=C.wb[name].t[:, c0:c1], in_=C.win32[name].t[:, c0:c1]),
                        appends=[C.wb[name]]))
        return todo
    carry = {}
    for j, (li, kind) in enumerate(plan):
        ks = []
        if j == 0:
            ks = [k for k in range(1, len(plan)) if plan[k][0] == li]
        elif kind == "ffn":
            ks = [k for k in range(j + 1, len(plan)) if plan[k][0] == li + 1 and plan[k][1] != "ffn"]
        elif kind == "mix":
            ks = [k for k in range(j + 1, len(plan)) if plan[k][0] == li and plan[k][1] == "ffn"]
        carry[j] = ks
    carried = set(k for ks in carry.values() for k in ks)
    for k in range(len(plan)):
        if k == 0 or k not in carried:
            for f in casts_for(k):
                f()

    jobs = []
    for j, (li, kind) in enumerate(plan):
        src, dst = bufs[j], bufs[j + 1]
        n0 = len(jobs)
        if kind == "ple":
            jobs += sub_ple(P, C, li, src, dst)
        elif kind == "ffn":
            jobs += sub_ffn(P, C, li, src, dst)
        elif li % 2 == 1:
            jobs += sub_odd(P, C, li, src, dst)
        else:
            jobs += sub_even(P, C, li, src, dst)
        todo = []
        for k in carry[j]:
            todo += casts_for(k)
        njobs = len(jobs) - n0
        for q, f in enumerate(todo):
            jobs[n0 + min(q * njobs // max(len(todo), 1), njobs - 1)].setdefault("casts", []).append(f)
    n = len(jobs)
    i = 0
    while i < n:
        if jobs[i]["kind"] == "even":
            jobs[i]["hT"] = hts[0]
            i += 1
            continue
        e = i
        while e < n and jobs[e]["kind"] != "even":
            e += 1
        for q in range(i, e):
            jobs[q]["hT"] = hts[(q - i) % 2]
        i = e

    def exhaust(g):
        if g is not None:
            for _ in g:
                pass
    C.pending = []
    g0 = jobs[0]["pre"](jobs[0])
    exhaust(g0)
    for j in range(n):
        g = jobs[j + 1]["pre"](jobs[j + 1]) if j + 1 < n else None

        late_b = (j + 1 < n and jobs[j + 1]["hT"] is jobs[j]["hT"] and jobs[j]["kind"] != "even")
        state = {"tag": "A", "done": g is None}

        def inject(k=1, g=g, late_b=late_b, state=state):
            for st_ in pend_prev:
                st_()
            del pend_prev[:]
            for _ in range(k):
                if state["done"] or (late_b and state["tag"] == "B"):
                    return
                try:
                    state["tag"] = next(g)
                except StopIteration:
                    state["done"] = True
        pend_prev = C.pending
        C.pending = []
        for f in jobs[j].get("casts", ()):
            f()
        jobs[j]["body"](jobs[j], inject)
        for st_ in pend_prev:
            st_()
        exhaust(g)
    for st_ in C.pending:
        st_()
    P.wait_all("sp", bufs[-1]["blk"])
    P.emit()
    return nc


def host_consts():
    bf = ml_dtypes.bfloat16
    out = {}
    s = np.arange(S, dtype=np.int64)
    idx = (s[:, None] * s[None, :]) % S
    ang = 2.0 * np.pi * np.arange(S, dtype=np.float64) / S
    ct = np.cos(ang).astype(np.float32)
    st = (-np.sin(ang)).astype(np.float32)

    def lay(tab):
        m = tab[idx]
        m = m.reshape(32, 128, 8, 512).transpose(1, 2, 0, 3)
        return np.ascontiguousarray(m).reshape(128, 8 * 32 * 512).astype(bf)
    out["dftc"] = lay(ct)
    out["dfts"] = lay(st)
    c = np.arange(128, dtype=np.float64)
    a = 2.0 * np.pi * np.outer(c, c) / 128
    sc = 1.0 / np.sqrt(float(S) * 128.0)
    out["cc"] = np.concatenate([np.cos(a) * sc, np.sin(a) * sc], axis=1).astype(np.float32).astype(bf)
    out["ident"] = np.eye(128, dtype=np.float32).astype(bf)
    return out


def host_weights(inp):
    w = {}
    vecs = np.zeros((128, DEPTH * NV), np.float32)
    for i in range(DEPTH):
        j = i // 2
        vb = i * NV

        def put(name, arr):
            vecs[:, vb + VEC[name]:vb + VEC[name] + arr.shape[1]] = arr
        put("mix_pre_g", _pvec(inp["mix_pre_g"][i]))
        put("mix_post_g", _pvec(inp["mix_post_g"][i]))
        put("ffn_pre_g", _pvec(inp["ffn_pre_g"][i]))
        put("ffn_post_g", _pvec(inp["ffn_post_g"][i]))
        put("ple_gate_g", _pvec(inp["ple_gate_g"][i]))
        put("ple_b_g", _pvec(inp["ple_b_g"][i]))
        for k in range(3):
            put(f"ffn_cw{k}", _pvec(inp["ffn_conv_w"][i][k]))
        put("ffn_cb", _pvec(inp["ffn_conv_b"][i]))
        if i % 2 == 0:
            put("ln_g", _pvec(inp["ev_v_ln_g"][j]))
            put("ln_b", _pvec(inp["ev_v_ln_b"][j]))
            win = np.asarray(inp["ev_w_in"][j], np.float32)
            w[f"wau{i}"] = _chunk_layout(win[:, :1024])
            w[f"wv{i}"] = np.ascontiguousarray(win[:, 1024:].reshape(8, 128, 512).transpose(1, 0, 2)).reshape(128, 4096)
            w[f"wf{i}"] = np.ascontiguousarray(np.asarray(inp["ev_w_fourier"][j], np.float32).transpose(1, 0, 2)).reshape(128, 512)
            w[f"wst{i}"] = np.ascontiguousarray(np.asarray(inp["ev_w_spatial"][j], np.float32).transpose(2, 0, 1)).reshape(128, 512)
            w[f"bs{i}"] = np.ascontiguousarray(np.asarray(inp["ev_b_spatial"][j], np.float32).reshape(1, 512))
            w[f"wo{i}"] = _chunk_layout(np.asarray(inp["ev_w_out"][j]))
        else:
            put("ln_g", _pvec(inp["od_ln_g"][j]))
            put("ln_b", _pvec(inp["od_ln_b"][j]))
            put("conv_b", _pvec(inp["od_conv_b"][j]))
            for k in range(3):
                put(f"sconv_w{k}", _pvec(inp["od_sconv_w"][j][k]))
            cw = np.asarray(inp["od_conv_w"][j], np.float32)
            for k in range(31):
                vecs[:, vb + VEC["conv_w"] + k * 4:vb + VEC["conv_w"] + k * 4 + 4] = _pvec(cw[k])
            w[f"win{i}"] = _chunk_layout(np.asarray(inp["od_w_in"][j]))
            w[f"wo{i}"] = _chunk_layout(np.asarray(inp["od_w_out"][j]))
        wup = np.asarray(inp["ffn_w_up"][i], np.float32)
        wup = wup.reshape(D, 2, NFC, 128).transpose(0, 2, 1, 3).reshape(D, 2 * DFF)
        w[f"wup{i}"] = _chunk_layout(wup)
        w[f"wdn{i}"] = _chunk_layout(np.asarray(inp["ffn_w_down"][i]))
        w[f"wg{i}"] = _chunk_layout(np.asarray(inp["ple_w_g"][i]))
        w[f"wp{i}"] = _chunk_layout(np.asarray(inp["ple_w_p"][i]))
    w["vecs"] = vecs
    return w


FULL_PLAN = [(li, k) for li in range(DEPTH) for k in ("mix", "ffn", "ple")]
_CACHE = {}


def run_plan(plan, inputs, xT_list, cores):
    key = tuple(plan)
    if key not in _CACHE:
        _CACHE[key] = build_program(plan)
    nc = _CACHE[key]
    hw = host_weights(inputs)
    hc = host_consts() if any(k == "mix" for _, k in plan) else {}
    kinds = set(plan)
    shared = {"vecs": hw["vecs"]}
    for name, _ in weight_specs():
        li = int(name[-1])
        kind = "mix" if name[:-1] in ("wau", "wv", "wf", "wst", "wo", "win") else ("ffn" if name[:-1] in ("wup", "wdn") else "ple")
        if (li, kind) in kinds:
            shared[name] = hw[name]
    for li in range(DEPTH):
        if li % 2 == 0 and (li, "mix") in kinds:
            shared[f"bs{li}"] = hw[f"bs{li}"]
    if any(l % 2 == 0 and k == "mix" for l, k in plan):
        shared["dftc"], shared["dfts"], shared["cc"] = hc["dftc"], hc["dfts"], hc["cc"]
    if any(l % 2 == 1 and k == "mix" for l, k in plan):
        shared["ident"] = hc["ident"]
    p = np.asarray(inputs["p"], np.float32)
    in_maps = []
    for ci, b in enumerate(cores):
        m = dict(shared)
        m["xT"] = xT_list[ci]
        m["pT"] = np.ascontiguousarray(p[:, b].transpose(0, 2, 1))
        in_maps.append(m)
    res = run_bass_kernel_spmd(nc, in_maps, core_ids=list(range(len(cores))))
    return [r["outT"] for r in res.results]


def kernel(**inputs):
    x = np.asarray(inputs["x"], np.float32)
    xT = [np.ascontiguousarray(x[b].T) for b in range(NCORES)]
    outs = run_plan(FULL_PLAN, inputs, xT, list(range(NCORES)))
    return np.stack([np.ascontiguousarray(o.T) for o in outs], axis=0).astype(np.float32)
```

```python
import numpy as np
import ml_dtypes
import concourse.bass as bass
import concourse.mybir as mybir
from concourse.bass_utils import run_bass_kernel_spmd
from contextlib import ExitStack

F32 = mybir.dt.float32
BF16 = mybir.dt.bfloat16
AF = mybir.ActivationFunctionType
ALU = mybir.AluOpType

D = 1024
S = 4096
DEPTH = 4
DFF = 2816
NFC = 22
PLE = 256
EPS = 1e-6
NCORES = 8

COMPUTE = ("pe", "act", "dve", "pool")
ENGS = ("pe", "act", "dve", "pool", "sp")
N_DMA_SEMS = 32


class T:
    __slots__ = ("name", "t", "w", "r")

    def __init__(self, name, t=None):
        self.name = name
        self.t = t
        self.w = []
        self.r = []

    def __getitem__(self, idx):
        return self.t[idx]


class Rot:
    def __init__(self, items):
        self.items = items
        self.i = 0

    def next(self):
        it = self.items[self.i % len(self.items)]
        self.i += 1
        return it


class Prog:
    def __init__(self, nc):
        self.nc = nc
        self.streams = {e: [] for e in ENGS}
        self.cnt = {e: 0 for e in COMPUTE}
        self.waited = {e: {} for e in ENGS}
        self.dma_i = 0
        self.es = ExitStack()

    def sbuf(self, name, shape, dtype):
        t = self.es.enter_context(self.nc.sbuf_tensor(name, list(shape), dtype))
        return T(name, t)

    def psum(self, name, shape, dtype=F32):
        t = self.es.enter_context(self.nc.psum_tensor(name, list(shape), dtype))
        return T(name, t)

    def dram(self, name, shape, dtype, kind="Internal"):
        t = self.nc.dram_tensor(name, list(shape), dtype, kind=kind)
        return T(name, t.ap())

    def _need(self, eng, ev, raw):
        semkey, val, src = ev
        if src == eng:
            if eng == "pe":
                return
            if not raw:
                return
        if self.waited[eng].get(semkey, 0) >= val:
            return
        self.waited[eng][semkey] = val
        self.streams[eng].append(("wait", semkey, val))

    def _deps(self, eng, reads, writes, appends):
        for t in reads:
            for ev in t.w:
                self._need(eng, ev, True)
        for t in writes:
            for ev in t.w:
                self._need(eng, ev, False)
            for ev in t.r:
                self._need(eng, ev, False)
        for t in appends:
            for ev in t.r:
                self._need(eng, ev, False)

    def _commit(self, ev, reads, writes, appends):
        semkey = ev[0]
        for t in reads:
            if not isinstance(semkey, tuple):
                t.r = [e for e in t.r if e[0] != semkey]
            t.r.append(ev)
        for t in writes:
            t.w = [ev]
            t.r = []
        for t in appends:
            if not isinstance(semkey, tuple):
                t.w = [e for e in t.w if e[0] != semkey]
            t.w.append(ev)
            t.r = []

    def op(self, eng, fn, reads=(), writes=(), appends=()):
        self._deps(eng, reads, writes, appends)
        self.cnt[eng] += 1
        ev = (eng, self.cnt[eng], eng)
        self.streams[eng].append(("op", fn, eng, 1))
        self._commit(ev, reads, writes, appends)
        return ev

    def mm(self, fns, reads=(), writes=(), appends=()):
        eng = "pe"
        self._deps(eng, reads, writes, appends)
        self.cnt[eng] += 1
        ev = (eng, self.cnt[eng], eng)
        for fn in fns[:-1]:
            self.streams[eng].append(("op", fn, None, 0))
        self.streams[eng].append(("op", fns[-1], eng, 1))
        self._commit(ev, reads, writes, appends)
        return ev

    def dma(self, q, fn, reads=(), writes=(), appends=()):
        self._deps(q, reads, writes, appends)
        i = self.dma_i
        self.dma_i += 1
        slot = i % N_DMA_SEMS
        k = i // N_DMA_SEMS
        semkey = ("dma", slot)
        if k > 0:
            self._need(q, (semkey, 16 * k, "dmaq"), True)
        ev = (semkey, 16 * (k + 1), "dmaq")
        self.streams[q].append(("op", fn, semkey, 16))
        self._commit(ev, reads, writes, appends)
        return ev

    def wait_all(self, eng, tiles):
        for t in tiles:
            for ev in t.w:
                self._need(eng, ev, True)

    def emit(self):
        nc = self.nc
        sems = {}
        for e in COMPUTE:
            sems[e] = self.es.enter_context(nc.semaphore("s_" + e))
        for s in range(N_DMA_SEMS):
            sems[("dma", s)] = self.es.enter_context(nc.semaphore("s_dma%d" % s))
        streams = self.streams

        def run(engobj, items):
            for it in items:
                if it[0] == "wait":
                    engobj.wait_ge(sems[it[1]], it[2])
                else:
                    _, fn, semkey, inc = it
                    ins = fn(engobj)
                    if inc:
                        ins.then_inc(sems[semkey], inc)

        with nc.Block() as block:
            @block.tensor
            def _(e):
                run(e, streams["pe"])

            @block.scalar
            def _(e):
                run(e, streams["act"])

            @block.vector
            def _(e):
                run(e, streams["dve"])

            @block.gpsimd
            def _(e):
                run(e, streams["pool"])

            @block.sync
            def _(e):
                run(e, streams["sp"])
        self.es.close()


VEC = {}
_off = 0
for _n, _k in [("mix_pre_g", 8), ("mix_post_g", 8), ("ffn_pre_g", 8), ("ffn_post_g", 8),
               ("ple_gate_g", 8), ("ple_b_g", 8), ("ffn_cw0", 44), ("ffn_cw1", 44),
               ("ffn_cw2", 44), ("ffn_cb", 44), ("ln_g", 4), ("ln_b", 4), ("conv_b", 4),
               ("sconv_w0", 4), ("sconv_w1", 4), ("sconv_w2", 4), ("conv_w", 124)]:
    VEC[_n] = _off
    _off += _k
NV = _off


def _pvec(v):
    return np.ascontiguousarray(np.asarray(v, np.float32).reshape(-1, 128).T)


def _chunk_layout(w):
    k, n = w.shape
    kc, oc = k // 128, n // 128
    return np.ascontiguousarray(
        np.asarray(w, np.float32).reshape(kc, 128, oc, 128).transpose(1, 2, 0, 3)).reshape(128, oc * kc * 128)


def weight_specs():
    specs = []
    for i in range(DEPTH):
        if i % 2 == 0:
            specs += [(f"wau{i}", 8 * 8 * 128), (f"wv{i}", 8 * 512), (f"wf{i}", 512), (f"wst{i}", 512),
                      (f"wo{i}", 8 * 8 * 128)]
        else:
            specs += [(f"win{i}", 20 * 8 * 128), (f"wo{i}", 8 * 8 * 128)]
        specs += [(f"wup{i}", 44 * 8 * 128), (f"wdn{i}", 8 * 22 * 128), (f"wg{i}", 8 * 8 * 128),
                  (f"wp{i}", 8 * 2 * 128)]
    return specs


class Ctx:
    pass


DEBUG_TILES = None
DEBUG_FFN = None


def tiles_for(halo, valid):
    out = []
    vs = 0
    while vs < S:
        wv = min(valid, S - vs)
        out.append((vs - halo, wv + 2 * halo, vs, wv))
        vs += wv
    if len(out) >= 2 and out[-1][3] < valid // 2:
        tot = out[-2][3] + out[-1][3]
        a = (tot // 2 + 1) // 2 * 2
        v0 = out[-2][2]
        out[-2:] = [(v0 - halo, a + 2 * halo, v0, a), (v0 + a - halo, tot - a + 2 * halo, v0 + a, tot - a)]
    if DEBUG_TILES is not None:
        out = [out[i] for i in DEBUG_TILES]
    return out


def xblocks(xb, a, b):
    a = max(a, 0)
    b = min(b, S)
    return [xb[i] for i in range(a // 512, (b - 1) // 512 + 1)]


class HT:
    def __init__(self, ap, deps):
        self.t = ap
        self.deps = deps

    def dep(self, c):
        return [self.deps[c]]

    def all(self):
        return list(dict.fromkeys(self.deps))


def load_x(P, C, src, s, W):
    xw = C.xw.next()
    a, b = max(s, 0), min(s + W, S)
    partial = (a > s or b < s + W)
    if a > s:
        P.op("pool", lambda e: e.memset(xw.t[:, :, 0:a - s], 0.0), writes=[xw])
    if b < s + W:
        P.op("pool", lambda e: e.memset(xw.t[:, :, b - s:W], 0.0), writes=[xw])
    sap = src["ap"].rearrange("(c p) t -> p c t", p=128)
    P.dma("sp", lambda e: e.dma_start(out=xw.t[:, :, a - s:b - s], in_=sap[:, :, a:b]),
          reads=xblocks(src["blk"], a, b), appends=[xw] if partial else (), writes=[] if partial else [xw])
    return xw


def store_x(P, C, dst, xw, off, vs, Wv):
    C.pending.append(lambda: _store_x(P, C, dst, xw, off, vs, Wv))


def _store_x(P, C, dst, xw, off, vs, Wv):
    dap = dst["ap"].rearrange("(c p) t -> p c t", p=128)
    P.dma("sp", lambda e: e.dma_start(out=dap[:, :, vs:vs + Wv], in_=xw.t[:, :, off:off + Wv]),
          reads=[xw], appends=xblocks(dst["blk"], vs, vs + Wv))


def pre_gen(P, C, job, src, s, W, gcol, prep=None):
    if prep is not None:
        prep()
    hT = job["hT"]
    xw = load_x(P, C, src, s, W)
    job["xw"] = xw
    st = C.ps_stat.next()

    def squares(c0):
        sqs = []
        for c in range(c0, c0 + 4):
            sq = C.sq.next()
            P.op("act", lambda e, c=c, sq=sq: e.activation(out=sq.t[:, :W], in_=xw.t[:, c, :W], func=AF.Square),
                 reads=[xw], writes=[sq])
            sqs.append(sq)
        return sqs

    def mms(c0, sqs):
        for i, sq in enumerate(sqs):
            c = c0 + i
            P.mm([lambda e, c=c, sq=sq: e.matmul(st.t[:, :W], lhsT=C.ones.t[:, :], rhs=sq.t[:, :W],
                                                 start=(c == 0), stop=(c == 7))],
                 reads=[sq, C.ones], writes=[st])
    s0 = squares(0)
    yield "A"
    mms(0, s0)
    yield "A"
    s1 = squares(4)
    yield "A"
    mms(4, s1)
    yield "B"
    rs = C.rs.next()
    P.op("act", lambda e: e.activation(out=rs.t[:, :W], in_=st.t[:, :W], func=AF.Sqrt, scale=1.0 / D, bias=C.epsb.t[:, 0:1]),
         reads=[st, C.epsb], writes=[rs])
    P.op("dve", lambda e: e.reciprocal(out=rs.t[:, :W], in_=rs.t[:, :W]), reads=[rs], writes=[rs])
    yield "B"
    for c in range(8):
        P.op("dve", lambda e, c=c: e.scalar_tensor_tensor(
            out=hT.t[:, c, :W], in0=xw.t[:, c, :W], scalar=C.vec.t[:, gcol + c:gcol + c + 1], in1=rs.t[:, :W],
            op0=ALU.mult, op1=ALU.mult), reads=[xw, rs, C.vec], writes=hT.dep(c))
    yield "E"


def wload(P, C, wb, col0, ncols):
    sl = C.wslots.next()
    first = True
    for a in range(0, ncols, 2048):
        b = min(ncols, a + 2048)
        P.dma("sp", lambda e, a=a, b=b: e.dma_start(out=sl.t[:, a:b], in_=wb.t[:, col0 + a:col0 + b]),
              reads=[wb], writes=[sl] if first else (), appends=() if first else [sl])
        first = False
    return sl


class PostNorm:
    def __init__(self, P, C, xw, off, Wv, gcol):
        self.P, self.C, self.xw, self.off, self.Wv, self.gcol = P, C, xw, off, Wv, gcol
        self.st = C.ps_stat.next()
        self.pend = None

    def _flush(self):
        if self.pend is not None:
            sq, first, last = self.pend
            st, Wv, C = self.st, self.Wv, self.C
            self.P.mm([lambda e: e.matmul(st.t[:, :Wv], lhsT=C.ones.t[:, :], rhs=sq.t[:, :Wv], start=first, stop=last)],
                      reads=[sq, C.ones], writes=[st])
            self.pend = None

    def chunk(self, dc, yb):
        P, C, Wv = self.P, self.C, self.Wv
        self._flush()
        sq = C.sq.next()
        P.op("act", lambda e: e.activation(out=sq.t[:, :Wv], in_=yb.t[:, :Wv], func=AF.Square), reads=[yb], writes=[sq])
        yg = C.yg[dc]
        g = self.gcol + dc
        P.op("act", lambda e: e.activation(out=yg.t[:, :Wv], in_=yb.t[:, :Wv], func=AF.Identity, scale=C.vec.t[:, g:g + 1]),
             reads=[yb, C.vec], writes=[yg])
        self.pend = (sq, dc == 0, dc == 7)

    def finish(self):
        P, C, Wv, xw, off = self.P, self.C, self.Wv, self.xw, self.off
        self._flush()
        st = self.st
        rs = C.rs.next()
        P.op("act", lambda e: e.activation(out=rs.t[:, :Wv], in_=st.t[:, :Wv], func=AF.Sqrt, scale=1.0 / D, bias=C.epsb.t[:, 0:1]),
             reads=[st, C.epsb], writes=[rs])
        P.op("dve", lambda e: e.reciprocal(out=rs.t[:, :Wv], in_=rs.t[:, :Wv]), reads=[rs], writes=[rs])
        for dc in range(8):
            yg = C.yg[dc]
            eng = "pool" if dc % 2 == 0 else "dve"
            P.op(eng, lambda e, yg=yg: e.tensor_tensor(out=yg.t[:, :Wv], in0=yg.t[:, :Wv], in1=rs.t[:, :Wv], op=ALU.mult),
                 reads=[rs, yg], writes=[yg])
            P.op("pool", lambda e, yg=yg, dc=dc: e.tensor_tensor(out=xw.t[:, dc, off:off + Wv], in0=xw.t[:, dc, off:off + Wv],
                                                                 in1=yg.t[:, :Wv], op=ALU.add),
                 reads=[yg], writes=[xw])


def out_proj(P, C, wo, mix, xw, off, vs, Wv, gcol, dst):
    pn = PostNorm(P, C, xw, off, Wv, gcol)
    for dc in range(8):
        wsl = wload(P, C, wo, dc * 1024, 1024)
        yb = C.ps.next()
        P.mm([lambda e, kc=kc, wsl=wsl, yb=yb: e.matmul(yb.t[:, :Wv], lhsT=wsl.t[:, kc * 128:(kc + 1) * 128], rhs=mix[kc].t[:, :Wv],
                                                       start=(kc == 0), stop=(kc == 7)) for kc in range(8)],
             reads=[wsl] + list(mix), writes=[yb])
        pn.chunk(dc, yb)
    pn.finish()
    store_x(P, C, dst, xw, off, vs, Wv)


def proj_fm(P, C, wb, oc, hT, W):
    wsl = wload(P, C, wb, oc * 1024, 1024)
    bk = C.ps.next()
    P.mm([lambda e, kc=kc: e.matmul(bk.t[:, :W], lhsT=wsl.t[:, kc * 128:(kc + 1) * 128], rhs=hT.t[:, kc, :W],
                                    start=(kc == 0), stop=(kc == 7)) for kc in range(8)],
         reads=[wsl] + hT.all(), writes=[bk])
    return bk


def sub_ple(P, C, li, src, dst):
    vb = li * NV
    wg, wp = C.wb[f"wg{li}"], C.wb[f"wp{li}"]
    jobs = []
    WG0, WP0 = 0, 16 * 512
    wg_blks = [C.big[i] for i in range(16)]
    wp_blks = [C.big[16 + i] for i in range(4)]

    def load_weights():
        for dc in range(8):
            for h in range(2):
                P.dma("sp", lambda e, dc=dc, h=h: e.dma_start(out=C.bigt[:, WG0 + dc * 1024 + h * 512:WG0 + dc * 1024 + (h + 1) * 512],
                                                             in_=wg.t[:, dc * 1024 + h * 512:dc * 1024 + (h + 1) * 512]),
                      reads=[wg], writes=[wg_blks[2 * dc + h]])
        for q in range(4):
            P.dma("sp", lambda e, q=q: e.dma_start(out=C.bigt[:, WP0 + q * 512:WP0 + (q + 1) * 512], in_=wp.t[:, q * 512:(q + 1) * 512]),
                  reads=[wp], writes=[wp_blks[q]])

    def mk(first, s, W, vs, Wv):
        def pre(job):
            def prep():
                pb = C.pb.next()
                job["pb"] = pb
                P.dma("sp", lambda e: e.dma_start(out=pb.t[:, :, :W], in_=C.pTb.t[li].rearrange("(c p) t -> p c t", p=128)[:, :, s:s + W]),
                      reads=[C.pTb_dep[li]], writes=[pb])
            return pre_gen(P, C, job, src, s, W, vb + VEC["ple_gate_g"], prep=prep)

        def body(job, inject):
            xw, hT, pb = job["xw"], job["hT"], job["pb"]
            if first:
                load_weights()
            for dc in range(8):
                gb = C.ps.next()
                P.mm([lambda e, kc=kc, gb=gb, dc=dc: e.matmul(
                    gb.t[:, :W], lhsT=C.bigt[:, WG0 + dc * 1024 + kc * 128:WG0 + dc * 1024 + (kc + 1) * 128], rhs=hT.t[:, kc, :W],
                    start=(kc == 0), stop=(kc == 7)) for kc in range(8)],
                    reads=[wg_blks[2 * dc], wg_blks[2 * dc + 1]] + hT.all(), writes=[gb])
                ppb = C.ps.next()
                P.mm([lambda e, kc=kc, ppb=ppb, dc=dc: e.matmul(
                    ppb.t[:, :W], lhsT=C.bigt[:, WP0 + dc * 256 + kc * 128:WP0 + dc * 256 + (kc + 1) * 128], rhs=pb.t[:, kc, :W],
                    start=(kc == 0), stop=(kc == 1)) for kc in range(2)],
                    reads=[wp_blks[dc // 2], pb], writes=[ppb])
                gt = C.f32.next()
                bcol = vb + VEC["ple_b_g"] + dc
                P.op("act", lambda e, gt=gt, gb=gb, bcol=bcol: e.activation(out=gt.t[:, :W], in_=gb.t[:, :W], func=AF.Sigmoid,
                                                                          bias=C.vec.t[:, bcol:bcol + 1]),
                     reads=[gb, C.vec], writes=[gt])
                P.op("dve", lambda e, gt=gt, ppb=ppb: e.tensor_tensor(out=gt.t[:, :W], in0=gt.t[:, :W], in1=ppb.t[:, :W], op=ALU.mult),
                     reads=[ppb, gt], writes=[gt])
                P.op("pool", lambda e, gt=gt, dc=dc: e.tensor_tensor(out=xw.t[:, dc, :W], in0=xw.t[:, dc, :W], in1=gt.t[:, :W], op=ALU.add),
                     reads=[gt], writes=[xw])
                if dc in (1, 2, 4):
                    inject(2)
            store_x(P, C, dst, xw, 0, vs, Wv)
        return {"kind": "ple", "pre": pre, "body": body}
    for i, tl_ in enumerate(tiles_for(0, 512)):
        jobs.append(mk(i == 0, *tl_))
    return jobs


def sub_ffn(P, C, li, src, dst):
    vb = li * NV
    wup, wdn = C.wb[f"wup{li}"], C.wb[f"wdn{li}"]
    jobs = []

    def mk(s, W, vs, Wv):
        def pre(job):
            return pre_gen(P, C, job, src, s, W, vb + VEC["ffn_pre_g"])

        def body(job, inject):
            xw, hT = job["xw"], job["hT"]
            acts = []
            pend2 = [None]
            for fc in range(NFC):
                wsl = wload(P, C, wup, fc * 2048, 2048)
                banks = []
                for h in range(2):
                    bk = C.ps.next()
                    P.mm([lambda e, kc=kc, h=h, wsl=wsl, bk=bk: e.matmul(
                        bk.t[:, :W], lhsT=wsl.t[:, h * 1024 + kc * 128:h * 1024 + (kc + 1) * 128], rhs=hT.t[:, kc, :W],
                        start=(kc == 0), stop=(kc == 7)) for kc in range(8)], reads=[wsl] + hT.all(), writes=[bk])
                    banks.append(bk)
                tmps = [C.f32.next(), C.f32.next()]
                cols = []
                for h in range(2):
                    ch = fc + h * NFC
                    cols.append((vb + VEC["ffn_cw0"] + ch, vb + VEC["ffn_cw1"] + ch, vb + VEC["ffn_cw2"] + ch, vb + VEC["ffn_cb"] + ch))
                for h in range(2):
                    bk, tt = banks[h], tmps[h]
                    c0, c1, c2, cb = cols[h]
                    P.op("act", lambda e, tt=tt, bk=bk, c1=c1, cb=cb: e.activation(
                        out=tt.t[:, :Wv], in_=bk.t[:, 1:1 + Wv], func=AF.Identity, scale=C.vec.t[:, c1:c1 + 1],
                        bias=C.vec.t[:, cb:cb + 1]), reads=[bk, C.vec], writes=[tt])
                for tap in (0, 2):
                    for h in range(2):
                        bk, tt = banks[h], tmps[h]
                        cc_ = cols[h][tap]
                        P.op("dve", lambda e, tt=tt, bk=bk, cc_=cc_, tap=tap: e.scalar_tensor_tensor(
                            out=tt.t[:, :Wv], in0=bk.t[:, tap:tap + Wv], scalar=C.vec.t[:, cc_:cc_ + 1], in1=tt.t[:, :Wv],
                            op0=ALU.mult, op1=ALU.add), reads=[bk, C.vec, tt], writes=[tt])
                def stage2(fc=fc, tmps=tmps):
                    P.op("act", lambda e, tt=tmps[0]: e.activation(out=tt.t[:, :Wv], in_=tt.t[:, :Wv], func=AF.Gelu),
                         reads=[tmps[0]], writes=[tmps[0]])
                    ab = C.big[fc]
                    P.op("pool", lambda e, fc=fc, a=tmps[0], b=tmps[1]: e.tensor_tensor(
                        out=C.bigt[:, fc * 512:fc * 512 + Wv], in0=a.t[:, :Wv], in1=b.t[:, :Wv], op=ALU.mult),
                        reads=[tmps[0], tmps[1]], writes=[ab])
                    acts.append(ab)
                if pend2[0] is not None:
                    pend2[0]()
                pend2[0] = stage2
                if fc == 10:
                    inject(0)
                elif fc in (13, 15, 16, 18):
                    inject(1)
            pend2[0]()
            pn = PostNorm(P, C, xw, 1, Wv, vb + VEC["ffn_post_g"])
            for dc in range(8):
                HF = NFC // 2
                wh = [wload(P, C, wdn, dc * NFC * 128, HF * 128), wload(P, C, wdn, dc * NFC * 128 + HF * 128, HF * 128)]
                yb = C.ps.next()
                P.mm([lambda e, fc=fc, wh=wh, yb=yb: e.matmul(
                    yb.t[:, :Wv], lhsT=wh[fc // HF].t[:, (fc % HF) * 128:(fc % HF + 1) * 128], rhs=C.bigt[:, fc * 512:fc * 512 + Wv],
                    start=(fc == 0), stop=(fc == NFC - 1)) for fc in range(NFC)], reads=wh + acts, writes=[yb])
                pn.chunk(dc, yb)
                if dc == 0:
                    inject(2)
            pn.finish()
            store_x(P, C, dst, xw, 1, vs, Wv)
        return {"kind": "ffn", "pre": pre, "body": body}
    for tl_ in tiles_for(1, 510):
        jobs.append(mk(*tl_))
    return jobs


def prep_odd(P, C, li):
    vb = li * NV
    for k in range(31):
        for g in range(4):
            idx = k * 4 + g
            col = vb + VEC["conv_w"] + idx
            blk = C.big[32 + idx // 4]
            P.op("pool", lambda e, idx=idx, col=col: e.tensor_scalar(
                out=C.bigt[:, 16384 + idx * 128:16384 + (idx + 1) * 128], in0=C.ident.t[:, :],
                scalar1=C.vec.t[:, col:col + 1], scalar2=0.0, op0=ALU.mult, op1=ALU.add),
                reads=[C.ident, C.vec], appends=[blk])


def sub_odd(P, C, li, src, dst):
    vb = li * NV
    win, wo = C.wb[f"win{li}"], C.wb[f"wo{li}"]
    diag_blks = [C.big[32 + i] for i in range(31)]
    H = 16
    jobs = []

    def mk(first, s, W, vs, Wv):
        def pre(job):
            return pre_gen(P, C, job, src, s, W, vb + VEC["mix_pre_g"], prep=(lambda: prep_odd(P, C, li)) if first else None)

        def body(job, inject):
            xw, hT = job["xw"], job["hT"]
            mix = [None] * 8
            st = [dict() for _ in range(4)]

            def S1(g):
                d = st[g]
                ab = proj_fm(P, C, win, g, hT, W)
                gb = proj_fm(P, C, win, 4 + g, hT, W)
                sg = C.f32.next()
                P.op("act", lambda e: e.activation(out=sg.t[:, :W], in_=gb.t[:, :W], func=AF.Sigmoid), reads=[gb], writes=[sg])
                gl = C.b16.next()
                P.op("dve", lambda e: e.tensor_tensor(out=gl.t[:, :W], in0=ab.t[:, :W], in1=sg.t[:, :W], op=ALU.mult),
                     reads=[ab, sg], writes=[gl])
                d.update(sg=sg, gl=gl)

            def S2(g):
                d = st[g]
                cgb = proj_fm(P, C, win, 12 + g, hT, W)
                xb_ = proj_fm(P, C, win, 16 + g, hT, W)
                xs = C.f32.next()
                P.op("act", lambda e: e.activation(out=xs.t[:, :W], in_=xb_.t[:, :W], func=AF.Copy), reads=[xb_], writes=[xs])
                P.op("dve", lambda e: e.tensor_tensor(out=xs.t[:, :W], in0=cgb.t[:, :W], in1=xs.t[:, :W], op=ALU.mult),
                     reads=[cgb, xs], writes=[xs])
                d.update(xs=xs)

            def S3(g):
                d = st[g]
                gl = d["gl"]
                cbk = C.ps.next()
                P.mm([lambda e, k=k: e.matmul(
                    cbk.t[:, :Wv], lhsT=C.bigt[:, 16384 + (k * 4 + g) * 128:16384 + (k * 4 + g + 1) * 128],
                    rhs=gl.t[:, H - 15 + k:H - 15 + k + Wv], start=(k == 0), stop=(k == 30)) for k in range(31)],
                    reads=[gl] + diag_blks, writes=[cbk])
                if g >= 1:
                    inject(2)
                cb = C.f32.next()
                bcol = vb + VEC["conv_b"] + g
                P.op("act", lambda e: e.activation(out=cb.t[:, :Wv], in_=cbk.t[:, :Wv], func=AF.Identity, bias=C.vec.t[:, bcol:bcol + 1]),
                     reads=[cbk, C.vec], writes=[cb])
                c16 = C.b16.next()
                P.op("act", lambda e: e.activation(out=c16.t[:, :Wv], in_=cbk.t[:, :Wv], func=AF.Identity, bias=C.vec.t[:, bcol:bcol + 1]),
                     reads=[cbk, C.vec], writes=[c16])
                d.update(cb=cb, c16=c16)

            def S4(g):
                d = st[g]
                cb, c16, xs = d["cb"], d["c16"], d["xs"]
                mb = C.ps.next()
                P.mm([lambda e: e.matmul(mb.t[:, :Wv], lhsT=C.onesdiv.t[:, :], rhs=c16.t[:, :Wv], start=True, stop=True)],
                     reads=[c16, C.onesdiv], writes=[mb])
                bb = proj_fm(P, C, win, 8 + g, hT, W)
                P.op("dve", lambda e: e.tensor_tensor(out=cb.t[:, :Wv], in0=cb.t[:, :Wv], in1=mb.t[:, :Wv], op=ALU.subtract),
                     reads=[mb, cb], writes=[cb])
                dsq = C.b16.next()
                P.op("act", lambda e: e.activation(out=dsq.t[:, :Wv], in_=cb.t[:, :Wv], func=AF.Square), reads=[cb], writes=[dsq])
                tt = C.f32.next()
                w0 = vb + VEC["sconv_w0"] + g
                P.op("dve", lambda e: e.tensor_scalar(out=tt.t[:, :Wv], in0=xs.t[:, H - 1:H - 1 + Wv],
                                                      scalar1=C.vec.t[:, w0:w0 + 1], scalar2=0.0, op0=ALU.mult, op1=ALU.add),
                     reads=[xs, C.vec], writes=[tt])
                d.update(bb=bb, dsq=dsq, tt=tt)

            def S5(g):
                d = st[g]
                sd, cb, xs, tt, bb, dsq = d["sg"], d["cb"], d["xs"], d["tt"], d["bb"], d["dsq"]
                w1, w2 = vb + VEC["sconv_w1"] + g, vb + VEC["sconv_w2"] + g
                vbk = C.ps.next()
                P.mm([lambda e: e.matmul(vbk.t[:, :Wv], lhsT=C.onesdiv.t[:, :], rhs=dsq.t[:, :Wv], start=True, stop=True)],
                     reads=[dsq, C.onesdiv], writes=[vbk])
                P.op("act", lambda e: e.activation(out=sd.t[:, :Wv], in_=vbk.t[:, :Wv], func=AF.Sqrt, bias=C.epsb.t[:, 0:1]),
                     reads=[vbk, C.epsb], writes=[sd])
                P.op("dve", lambda e: e.scalar_tensor_tensor(
                    out=tt.t[:, :Wv], in0=xs.t[:, H:H + Wv], scalar=C.vec.t[:, w1:w1 + 1], in1=tt.t[:, :Wv],
                    op0=ALU.mult, op1=ALU.add), reads=[xs, C.vec, tt], writes=[tt])
                P.op("dve", lambda e: e.reciprocal(out=sd.t[:, :Wv], in_=sd.t[:, :Wv]), reads=[sd], writes=[sd])
                P.op("dve", lambda e: e.scalar_tensor_tensor(
                    out=tt.t[:, :Wv], in0=xs.t[:, H + 1:H + 1 + Wv], scalar=C.vec.t[:, w2:w2 + 1], in1=tt.t[:, :Wv],
                    op0=ALU.mult, op1=ALU.add), reads=[xs, C.vec, tt], writes=[tt])
                P.op("dve", lambda e: e.tensor_tensor(out=cb.t[:, :Wv], in0=cb.t[:, :Wv], in1=sd.t[:, :Wv], op=ALU.mult),
                     reads=[sd, cb], writes=[cb])
                mx = C.mix[g]
                gcol, bcol2 = vb + VEC["ln_g"] + g, vb + VEC["ln_b"] + g
                P.op("act", lambda e: e.activation(out=mx.t[:, :Wv], in_=cb.t[:, :Wv], func=AF.Silu, scale=C.vec.t[:, gcol:gcol + 1],
                                                   bias=C.vec.t[:, bcol2:bcol2 + 1]), reads=[cb, C.vec], writes=[mx])
                mix[g] = mx
                mx2 = C.mix[4 + g]
                P.op("dve", lambda e: e.tensor_tensor(out=mx2.t[:, :Wv], in0=bb.t[:, H:H + Wv], in1=tt.t[:, :Wv], op=ALU.mult),
                     reads=[bb, tt], writes=[mx2])
                mix[4 + g] = mx2

            S1(0)
            S2(0)
            S3(0)
            for g in range(1, 4):
                S1(g)
                S4(g - 1)
                S2(g)
                S5(g - 1)
                S3(g)
            S4(3)
            S5(3)
            out_proj(P, C, wo, mix, xw, H, vs, Wv, vb + VEC["mix_post_g"], dst)
        return {"kind": "odd", "pre": pre, "body": body}
    for i, tl_ in enumerate(tiles_for(H, 480)):
        jobs.append(mk(i == 0, *tl_))
    return jobs


def prep_even(P, C, li):
    vb = li * NV
    wv, wf, wst = C.wb[f"wv{li}"], C.wb[f"wf{li}"], C.wb[f"wst{li}"]
    bs = C.bs[li]
    wfs = wload(P, C, wf, 0, 512)
    wcs = C.wcs
    for g in range(4):
        bk = C.ps.next()
        P.mm([lambda e, g=g, bk=bk: e.matmul(bk.t[:, 0:128], lhsT=C.cc.t[:, 0:128], rhs=wfs.t[:, g * 128:(g + 1) * 128],
                                             start=True, stop=True),
              lambda e, g=g, bk=bk: e.matmul(bk.t[:, 128:256], lhsT=C.cc.t[:, 128:256], rhs=wfs.t[:, g * 128:(g + 1) * 128],
                                             start=True, stop=True)], reads=[wfs, C.cc], writes=[bk])
        P.op("act", lambda e, g=g, bk=bk: e.activation(out=wcs.t[:, g * 256:(g + 1) * 256], in_=bk.t[:, 0:256], func=AF.Copy),
             reads=[bk], appends=[wcs])
    wsts = C.wsts
    P.dma("sp", lambda e: e.dma_start(out=wsts.t[:, :], in_=wst.t[:, :]), reads=[wst], writes=[wsts])
    wst32 = C.f32.next()
    P.dma("sp", lambda e: e.dma_start(out=wst32.t[:, :], in_=C.win32[f"wst{li}"].t[:, :]), reads=[], writes=[wst32])
    bs0 = C.f32.next()
    P.op("pool", lambda e: e.memset(bs0.t[:, :], 0.0), writes=[bs0])
    P.dma("sp", lambda e: e.dma_start(out=bs0.t[0:1, :], in_=bs.t[0:1, :]), reads=[], appends=[bs0])
    bias2 = C.bias2
    for g in range(4):
        bk = C.ps.next()
        P.mm([lambda e, g=g, bk=bk: e.matmul(bk.t[:, 0:128], lhsT=C.ones32.t[:, :], rhs=wst32.t[:, g * 128:(g + 1) * 128],
                                             start=True, stop=True),
              lambda e, g=g, bk=bk: e.matmul(bk.t[:, 128:256], lhsT=C.ones32.t[:, :], rhs=bs0.t[:, g * 128:(g + 1) * 128],
                                             start=True, stop=True)], reads=[wst32, bs0, C.ones32], writes=[bk])
        lb = vb + VEC["ln_b"] + g
        tmp = C.f32.next()
        P.op("act", lambda e, bk=bk, tmp=tmp: e.activation(out=tmp.t[:, 0:128], in_=bk.t[:, 128:256], func=AF.Copy),
             reads=[bk], writes=[tmp])
        P.op("dve", lambda e, g=g, bk=bk, lb=lb, tmp=tmp: e.scalar_tensor_tensor(
            out=bias2.t[:, g * 128:(g + 1) * 128], in0=bk.t[:, 0:128], scalar=C.vec.t[:, lb:lb + 1], in1=tmp.t[:, 0:128],
            op0=ALU.mult, op1=ALU.add), reads=[bk, tmp, C.vec], appends=[bias2])


def sub_even(P, C, li, src, dst):
    vb = li * NV
    wau, wo = C.wb[f"wau{li}"], C.wb[f"wo{li}"]
    wcs, wsts, bias2 = C.wcs, C.wsts, C.bias2
    wv = C.wb[f"wv{li}"]
    tl = tiles_for(0, 512)
    jobs = []

    def mk1(ti, s, W, vs, Wv):
        def pre(job):
            return pre_gen(P, C, job, src, s, W, vb + VEC["mix_pre_g"], prep=(lambda: prep_even(P, C, li)) if ti == 0 else None)

        def body(job, inject):
            hT = job["hT"]
            ats = []
            for g in range(4):
                bk = proj_fm(P, C, wau, g, hT, W)
                at = C.b16.next()
                P.op("act", lambda e, at=at, bk=bk: e.activation(out=at.t[:, :W], in_=bk.t[:, :W], func=AF.Copy), reads=[bk], writes=[at])
                ats.append(at)
            inject(2)
            for ts_ in range(4):
                sc = ti * 4 + ts_
                for half in range(2):
                    bk = C.ps.next()
                    P.mm([lambda e, g=g, ts_=ts_, bk=bk, j=j: e.matmul(
                        bk.t[:, j * 256:(j + 1) * 256], lhsT=ats[g].t[:, ts_ * 128:(ts_ + 1) * 128], rhs=wcs.t[:, g * 256:(g + 1) * 256],
                        start=True, stop=True) for j, g in enumerate((2 * half, 2 * half + 1))],
                        reads=ats + [wcs], writes=[bk])
                    blk = C.big[2 * sc + half]
                    if half == 0:
                        P.op("act", lambda e, sc=sc, half=half, bk=bk: e.activation(
                            out=C.bigt[:, sc * 1024 + half * 512:sc * 1024 + (half + 1) * 512], in_=bk.t[:, :], func=AF.Copy),
                            reads=[bk], writes=[blk])
                    else:
                        P.op("dve", lambda e, sc=sc, half=half, bk=bk: e.tensor_copy(
                            out=C.bigt[:, sc * 1024 + half * 512:sc * 1024 + (half + 1) * 512], in_=bk.t[:, :]),
                            reads=[bk], writes=[blk])
                if ts_ in (0, 2):
                    inject(2)
        return {"kind": "even", "pre": pre, "body": body}
    for ti, tl_ in enumerate(tl):
        jobs.append(mk1(ti, *tl_))

    allB = [C.big[i] for i in range(64)]

    def mk2(ti, s, W, vs, Wv):
        def pre(job):
            return pre_gen(P, C, job, src, s, W, vb + VEC["mix_pre_g"])

        def body(job, inject):
            xw, hT = job["xw"], job["hT"]
            vns = []
            wvh = [wload(P, C, wv, 0, 2048), wload(P, C, wv, 2048, 2048)]
            for ts_ in range(4):
                bk = C.ps.next()
                P.mm([lambda e, kc=kc, ts_=ts_, bk=bk: e.matmul(bk.t[:, :], lhsT=hT.t[:, kc, ts_ * 128:(ts_ + 1) * 128],
                                                               rhs=wvh[kc // 4].t[:, (kc % 4) * 512:(kc % 4 + 1) * 512],
                                                               start=(kc == 0), stop=(kc == 7))
                      for kc in range(8)], reads=hT.all() + wvh, writes=[bk])
                vg = C.f32.next()
                P.op("act", lambda e, vg=vg, bk=bk: e.activation(out=vg.t[:, :], in_=bk.t[:, :], func=AF.Gelu), reads=[bk], writes=[vg])
                stt = C.stt.next()
                for g in range(4):
                    P.op("dve", lambda e, g=g, vg=vg, stt=stt: e.bn_stats(out=stt.t[:, g * 6:(g + 1) * 6], in_=vg.t[:, g * 128:(g + 1) * 128]),
                         reads=[vg], appends=[stt])
                mv = C.mv.next()
                for g in range(4):
                    P.op("dve", lambda e, g=g, mv=mv, stt=stt: e.bn_aggr(out=mv.t[:, g * 2:(g + 1) * 2], in_=stt.t[:, g * 6:(g + 1) * 6]),
                         reads=[stt], appends=[mv])
                rr = C.rr.next()
                P.op("act", lambda e, rr=rr, mv=mv: e.activation(out=rr.t[:, 0:4], in_=mv.t[:, 1:8:2], func=AF.Sqrt, bias=C.epsb.t[:, 0:1]),
                     reads=[mv, C.epsb], writes=[rr])
                P.op("dve", lambda e, rr=rr: e.reciprocal(out=rr.t[:, 0:4], in_=rr.t[:, 0:4]), reads=[rr], writes=[rr])
                vn = C.b16.next()
                for g in range(4):
                    P.op("dve", lambda e, g=g, vn=vn, vg=vg, mv=mv, rr=rr: e.tensor_scalar(
                        out=vn.t[:, g * 128:(g + 1) * 128], in0=vg.t[:, g * 128:(g + 1) * 128], scalar1=mv.t[:, 2 * g:2 * g + 1],
                        scalar2=rr.t[:, g:g + 1], op0=ALU.subtract, op1=ALU.mult), reads=[vg, mv, rr], appends=[vn])
                vns.append(vn)
            uts = []
            for g in range(4):
                bk = proj_fm(P, C, wau, 4 + g, hT, W)
                ut = C.ut[g]
                P.op("act", lambda e, ut=ut, bk=bk: e.activation(out=ut.t[:, :W], in_=bk.t[:, :W], func=AF.Gelu), reads=[bk], writes=[ut])
                uts.append(ut)
            inject(2)
            for ts_ in range(4):
                vn = vns[ts_]
                mbk = C.ps.next()
                P.mm([lambda e, g=g, vn=vn, mbk=mbk: e.matmul(mbk.t[:, g * 128:(g + 1) * 128], lhsT=vn.t[:, g * 128:(g + 1) * 128],
                                                             rhs=wsts.t[:, g * 128:(g + 1) * 128], start=True, stop=True) for g in range(4)],
                     reads=[vn, wsts], writes=[mbk])
                for g in range(4):
                    tt = C.f32.next()
                    lg = vb + VEC["ln_g"] + g
                    P.op("dve", lambda e, g=g, tt=tt, mbk=mbk, lg=lg: e.scalar_tensor_tensor(
                        out=tt.t[:, 0:128], in0=mbk.t[:, g * 128:(g + 1) * 128], scalar=C.vec.t[:, lg:lg + 1],
                        in1=bias2.t[:, g * 128:(g + 1) * 128], op0=ALU.mult, op1=ALU.add), reads=[mbk, bias2, C.vec], writes=[tt])
                    mx = C.mix[4 + g]
                    P.op("pool", lambda e, g=g, tt=tt, mx=mx, ts_=ts_: e.tensor_tensor(
                        out=mx.t[:, ts_ * 128:(ts_ + 1) * 128], in0=tt.t[:, 0:128], in1=uts[g].t[:, ts_ * 128:(ts_ + 1) * 128],
                        op=ALU.mult), reads=[tt, uts[g]], appends=[mx])
            fb = [C.ps.next() for _ in range(4)]
            for sb in range(8):
                col0 = ti * 16384 + sb * 2048
                csl = wload(P, C, C.dftc, col0, 2048)
                ssl = wload(P, C, C.dfts, col0, 2048)
                fns = []
                for g in range(4):
                    for q in range(4):
                        sc = sb * 4 + q
                        fns.append(lambda e, g=g, q=q, sc=sc, csl=csl: e.matmul(
                            fb[g].t[:, :], lhsT=C.bigt[:, sc * 1024 + g * 256:sc * 1024 + g * 256 + 128], rhs=csl.t[:, q * 512:(q + 1) * 512],
                            start=(sc == 0), stop=False))
                        fns.append(lambda e, g=g, q=q, sc=sc, ssl=ssl: e.matmul(
                            fb[g].t[:, :], lhsT=C.bigt[:, sc * 1024 + g * 256 + 128:sc * 1024 + g * 256 + 256], rhs=ssl.t[:, q * 512:(q + 1) * 512],
                            start=False, stop=(sc == 31)))
                P.mm(fns, reads=[csl, ssl] + allB, writes=fb)
                if sb in (1, 3):
                    inject(2)
            mix = []
            for g in range(4):
                mx = C.mix[g]
                P.op("act", lambda e, g=g, mx=mx: e.activation(out=mx.t[:, :], in_=fb[g].t[:, :], func=AF.Copy), reads=[fb[g]], writes=[mx])
                mix.append(mx)
            mix += [C.mix[4 + g] for g in range(4)]
            out_proj(P, C, wo, mix, xw, 0, vs, Wv, vb + VEC["mix_post_g"], dst)
        return {"kind": "even", "pre": pre, "body": body}
    for ti, tl_ in enumerate(tl):
        jobs.append(mk2(ti, *tl_))
    return jobs


def build_program(plan, need_dft=True):
    nc = bass.Bass("TRN2", target_bir_lowering=False)
    P = Prog(nc)
    C = Ctx()
    xin = P.dram("xT", [D, S], F32, kind="ExternalInput")
    C.pT = P.dram("pT", [DEPTH, PLE, S], F32, kind="ExternalInput")
    vecs = P.dram("vecs", [128, DEPTH * NV], F32, kind="ExternalInput")
    out = P.dram("outT", [D, S], F32, kind="ExternalOutput")
    layers = sorted(set(l for l, _ in plan))
    kinds = set(plan)
    C.win32, C.wb, C.bs = {}, {}, {}
    used = []

    def wkind(name):
        return "mix" if name[:-1] in ("wau", "wv", "wf", "wst", "wo", "win") else ("ffn" if name[:-1] in ("wup", "wdn") else "ple")
    for name, ncols in weight_specs():
        li = int(name[-1])
        if (li, wkind(name)) not in kinds:
            continue
        C.win32[name] = P.dram(name, [128, ncols], F32, kind="ExternalInput")
        C.wb[name] = P.dram(name + "_b", [128, ncols], BF16)
        used.append((name, ncols))
    for li in layers:
        if li % 2 == 0 and (li, "mix") in kinds:
            C.bs[li] = P.dram(f"bs{li}", [1, 512], F32, kind="ExternalInput")
    has_even = any(l % 2 == 0 and k == "mix" for l, k in plan)
    has_odd = any(l % 2 == 1 and k == "mix" for l, k in plan)
    if has_even:
        C.dftc = P.dram("dftc", [128, 8 * 32 * 512], BF16, kind="ExternalInput")
        C.dfts = P.dram("dfts", [128, 8 * 32 * 512], BF16, kind="ExternalInput")
        ccd = P.dram("cc", [128, 256], BF16, kind="ExternalInput")
    if has_odd:
        identd = P.dram("ident", [128, 128], BF16, kind="ExternalInput")
    xa = P.dram("xa", [D, S], F32)
    xb = P.dram("xb", [D, S], F32)
    C.pTb = P.dram("pTb", [DEPTH, PLE, S], BF16)
    C.pTb_dep = [T("pTb%d" % i) for i in range(DEPTH)]

    def xbuf(t):
        return {"ap": t.t, "blk": [T(t.name + "_b%d" % i) for i in range(8)]}
    bufs = [xbuf(xin)] + [None] * (len(plan) - 1)
    xab = [xbuf(xa), xbuf(xb)]
    for j in range(1, len(plan)):
        bufs[j] = xab[j % 2]
    bufs.append(xbuf(out))

    C.vec = P.sbuf("vec", [128, DEPTH * NV], F32)
    C.ones = P.sbuf("ones", [128, 128], BF16)
    C.onesdiv = P.sbuf("onesdiv", [128, 128], BF16)
    C.epsb = P.sbuf("epsb", [128, 1], F32)
    C.xw = Rot([P.sbuf("xw%d" % i, [128, 8, 512], F32) for i in range(2)])
    hT0 = P.sbuf("hT0", [128, 8, 512], BF16)
    C.sq = Rot([P.sbuf("sq%d" % i, [128, 512], BF16) for i in range(4)])
    C.rs = Rot([P.sbuf("rs%d" % i, [128, 512], F32) for i in range(2)])
    C.yg = [P.sbuf("yg%d" % i, [128, 512], F32) for i in range(8)]
    C.wslots = Rot([P.sbuf("wsl%d" % i, [128, 2048], BF16) for i in range(6)])
    bigt = P.sbuf("big", [128, 32768], BF16)
    C.bigt = bigt.t
    C.big = [T("big%d" % i) for i in range(64)]
    hts = [HT(hT0.t, [hT0] * 8),
           HT(C.bigt[:, 12288:16384].rearrange("p (c w) -> p c w", c=8), [C.big[24 + c] for c in range(8)])]
    C.f32 = Rot([P.sbuf("f32_%d" % i, [128, 512], F32) for i in range(8)])
    C.b16 = Rot([P.sbuf("b16_%d" % i, [128, 512], BF16) for i in range(8)])
    C.mix = [P.sbuf("mix%d" % i, [128, 512], BF16) for i in range(8)]
    C.pb = Rot([P.sbuf("pb%d" % i, [128, 2, 512], BF16) for i in range(2)])
    if has_even:
        C.cc = P.sbuf("cc_s", [128, 256], BF16)
        C.wcs = P.sbuf("wcs", [128, 1024], BF16)
        C.wsts = P.sbuf("wsts", [128, 512], BF16)
        C.bias2 = P.sbuf("bias2", [128, 512], F32)
        C.ones32 = P.sbuf("ones32", [128, 128], F32)
        C.ut = [P.sbuf("ut%d" % i, [128, 512], BF16) for i in range(4)]
        C.stt = Rot([P.sbuf("stt%d" % i, [128, 24], F32) for i in range(4)])
        C.mv = Rot([P.sbuf("mv%d" % i, [128, 8], F32) for i in range(4)])
        C.rr = Rot([P.sbuf("rr%d" % i, [128, 4], F32) for i in range(4)])
    if has_odd:
        C.ident = P.sbuf("ident_s", [128, 128], BF16)
    C.ps = Rot([P.psum("ps%d" % i, [128, 512]) for i in range(6)])
    C.ps_stat = Rot([P.psum("pst%d" % i, [128, 512]) for i in range(2)])

    P.op("pool", lambda e: e.memset(C.ones.t[:, :], 1.0), writes=[C.ones])
    P.op("pool", lambda e: e.memset(C.onesdiv.t[:, :], 1.0 / 128), writes=[C.onesdiv])
    P.op("pool", lambda e: e.memset(C.epsb.t[:, :], EPS), writes=[C.epsb])
    P.dma("sp", lambda e: e.dma_start(out=C.vec.t[:, :], in_=vecs.t[:, :]), writes=[C.vec])
    if has_even:
        P.op("pool", lambda e: e.memset(C.ones32.t[:, :], 1.0), writes=[C.ones32])
        P.dma("sp", lambda e: e.dma_start(out=C.cc.t[:, :], in_=ccd.t[:, :]), writes=[C.cc])
    if has_odd:
        P.dma("sp", lambda e: e.dma_start(out=C.ident.t[:, :], in_=identd.t[:, :]), writes=[C.ident])

    def casts_for(k):
        li, kind = plan[k]
        todo = []
        for name, ncols in used:
            if int(name[-1]) == li and wkind(name) == kind:
                if name.startswith("wg"):
                    for h in range(2):
                        todo.append(lambda li=li, h=h: P.dma(
                            "pool", lambda e: e.dma_start(out=C.pTb.t[li, :, h * 2048:(h + 1) * 2048], in_=C.pT.t[li, :, h * 2048:(h + 1) * 2048]),
                            appends=[C.pTb_dep[li]]))
                step = 8192
                for c0 in range(0, ncols, step):
                    c1 = min(ncols, c0 + step)
                    todo.append(lambda name=name, c0=c0, c1=c1: P.dma(
                        "pool", lambda e: e.dma_start(out=C.wb[name].t[:, c0:c1], in_=C.win32[name].t[:, c0:c1]),
                        appends=[C.wb[name]]))
        return todo
    carry = {}
    for j, (li, kind) in enumerate(plan):
        ks = []
        if j == 0:
            ks = [k for k in range(1, len(plan)) if plan[k][0] == li]
        elif kind == "ffn":
            ks = [k for k in range(j + 1, len(plan)) if plan[k][0] == li + 1 and plan[k][1] != "ffn"]
        elif kind == "mix":
            ks = [k for k in range(j + 1, len(plan)) if plan[k][0] == li and plan[k][1] == "ffn"]
        carry[j] = ks
    carried = set(k for ks in carry.values() for k in ks)
    for k in range(len(plan)):
        if k == 0 or k not in carried:
            for f in casts_for(k):
                f()

    jobs = []
    for j, (li, kind) in enumerate(plan):
        src, dst = bufs[j], bufs[j + 1]
        n0 = len(jobs)
        if kind == "ple":
            jobs += sub_ple(P, C, li, src, dst)
        elif kind == "ffn":
            jobs += sub_ffn(P, C, li, src, dst)
        elif li % 2 == 1:
            jobs += sub_odd(P, C, li, src, dst)
        else:
            jobs += sub_even(P, C, li, src, dst)
        todo = []
        for k in carry[j]:
            todo += casts_for(k)
        njobs = len(jobs) - n0
        for q, f in enumerate(todo):
            jobs[n0 + min(q * njobs // max(len(todo), 1), njobs - 1)].setdefault("casts", []).append(f)
    n = len(jobs)
    i = 0
    while i < n:
        if jobs[i]["kind"] == "even":
            jobs[i]["hT"] = hts[0]
            i += 1
            continue
        e = i
        while e < n and jobs[e]["kind"] != "even":
            e += 1
        for q in range(i, e):
            jobs[q]["hT"] = hts[(q - i) % 2]
        i = e

    def exhaust(g):
        if g is not None:
            for _ in g:
                pass
    C.pending = []
    g0 = jobs[0]["pre"](jobs[0])
    exhaust(g0)
    for j in range(n):
        g = jobs[j + 1]["pre"](jobs[j + 1]) if j + 1 < n else None

        late_b = (j + 1 < n and jobs[j + 1]["hT"] is jobs[j]["hT"] and jobs[j]["kind"] != "even")
        state = {"tag": "A", "done": g is None}

        def inject(k=1, g=g, late_b=late_b, state=state):
            for st_ in pend_prev:
                st_()
            del pend_prev[:]
            for _ in range(k):
                if state["done"] or (late_b and state["tag"] == "B"):
                    return
                try:
                    state["tag"] = next(g)
                except StopIteration:
                    state["done"] = True
        pend_prev = C.pending
        C.pending = []
        for f in jobs[j].get("casts", ()):
            f()
        jobs[j]["body"](jobs[j], inject)
        for st_ in pend_prev:
            st_()
        exhaust(g)
    for st_ in C.pending:
        st_()
    P.wait_all("sp", bufs[-1]["blk"])
    P.emit()
    return nc


def host_consts():
    bf = ml_dtypes.bfloat16
    out = {}
    s = np.arange(S, dtype=np.int64)
    idx = (s[:, None] * s[None, :]) % S
    ang = 2.0 * np.pi * np.arange(S, dtype=np.float64) / S
    ct = np.cos(ang).astype(np.float32)
    st = (-np.sin(ang)).astype(np.float32)

    def lay(tab):
        m = tab[idx]
        m = m.reshape(32, 128, 8, 512).transpose(1, 2, 0, 3)
        return np.ascontiguousarray(m).reshape(128, 8 * 32 * 512).astype(bf)
    out["dftc"] = lay(ct)
    out["dfts"] = lay(st)
    c = np.arange(128, dtype=np.float64)
    a = 2.0 * np.pi * np.outer(c, c) / 128
    sc = 1.0 / np.sqrt(float(S) * 128.0)
    out["cc"] = np.concatenate([np.cos(a) * sc, np.sin(a) * sc], axis=1).astype(np.float32).astype(bf)
    out["ident"] = np.eye(128, dtype=np.float32).astype(bf)
    return out


def host_weights(inp):
    w = {}
    vecs = np.zeros((128, DEPTH * NV), np.float32)
    for i in range(DEPTH):
        j = i // 2
        vb = i * NV

        def put(name, arr):
            vecs[:, vb + VEC[name]:vb + VEC[name] + arr.shape[1]] = arr
        put("mix_pre_g", _pvec(inp["mix_pre_g"][i]))
        put("mix_post_g", _pvec(inp["mix_post_g"][i]))
        put("ffn_pre_g", _pvec(inp["ffn_pre_g"][i]))
        put("ffn_post_g", _pvec(inp["ffn_post_g"][i]))
        put("ple_gate_g", _pvec(inp["ple_gate_g"][i]))
        put("ple_b_g", _pvec(inp["ple_b_g"][i]))
        for k in range(3):
            put(f"ffn_cw{k}", _pvec(inp["ffn_conv_w"][i][k]))
        put("ffn_cb", _pvec(inp["ffn_conv_b"][i]))
        if i % 2 == 0:
            put("ln_g", _pvec(inp["ev_v_ln_g"][j]))
            put("ln_b", _pvec(inp["ev_v_ln_b"][j]))
            win = np.asarray(inp["ev_w_in"][j], np.float32)
            w[f"wau{i}"] = _chunk_layout(win[:, :1024])
            w[f"wv{i}"] = np.ascontiguousarray(win[:, 1024:].reshape(8, 128, 512).transpose(1, 0, 2)).reshape(128, 4096)
            w[f"wf{i}"] = np.ascontiguousarray(np.asarray(inp["ev_w_fourier"][j], np.float32).transpose(1, 0, 2)).reshape(128, 512)
            w[f"wst{i}"] = np.ascontiguousarray(np.asarray(inp["ev_w_spatial"][j], np.float32).transpose(2, 0, 1)).reshape(128, 512)
            w[f"bs{i}"] = np.ascontiguousarray(np.asarray(inp["ev_b_spatial"][j], np.float32).reshape(1, 512))
            w[f"wo{i}"] = _chunk_layout(np.asarray(inp["ev_w_out"][j]))
        else:
            put("ln_g", _pvec(inp["od_ln_g"][j]))
            put("ln_b", _pvec(inp["od_ln_b"][j]))
            put("conv_b", _pvec(inp["od_conv_b"][j]))
            for k in range(3):
                put(f"sconv_w{k}", _pvec(inp["od_sconv_w"][j][k]))
            cw = np.asarray(inp["od_conv_w"][j], np.float32)
            for k in range(31):
                vecs[:, vb + VEC["conv_w"] + k * 4:vb + VEC["conv_w"] + k * 4 + 4] = _pvec(cw[k])
            w[f"win{i}"] = _chunk_layout(np.asarray(inp["od_w_in"][j]))
            w[f"wo{i}"] = _chunk_layout(np.asarray(inp["od_w_out"][j]))
        wup = np.asarray(inp["ffn_w_up"][i], np.float32)
        wup = wup.reshape(D, 2, NFC, 128).transpose(0, 2, 1, 3).reshape(D, 2 * DFF)
        w[f"wup{i}"] = _chunk_layout(wup)
        w[f"wdn{i}"] = _chunk_layout(np.asarray(inp["ffn_w_down"][i]))
        w[f"wg{i}"] = _chunk_layout(np.asarray(inp["ple_w_g"][i]))
        w[f"wp{i}"] = _chunk_layout(np.asarray(inp["ple_w_p"][i]))
    w["vecs"] = vecs
    return w


FULL_PLAN = [(li, k) for li in range(DEPTH) for k in ("mix", "ffn", "ple")]
_CACHE = {}


def run_plan(plan, inputs, xT_list, cores):
    key = tuple(plan)
    if key not in _CACHE:
        _CACHE[key] = build_program(plan)
    nc = _CACHE[key]
    hw = host_weights(inputs)
    hc = host_consts() if any(k == "mix" for _, k in plan) else {}
    kinds = set(plan)
    shared = {"vecs": hw["vecs"]}
    for name, _ in weight_specs():
        li = int(name[-1])
        kind = "mix" if name[:-1] in ("wau", "wv", "wf", "wst", "wo", "win") else ("ffn" if name[:-1] in ("wup", "wdn") else "ple")
        if (li, kind) in kinds:
            shared[name] = hw[name]
    for li in range(DEPTH):
        if li % 2 == 0 and (li, "mix") in kinds:
            shared[f"bs{li}"] = hw[f"bs{li}"]
    if any(l % 2 == 0 and k == "mix" for l, k in plan):
        shared["dftc"], shared["dfts"], shared["cc"] = hc["dftc"], hc["dfts"], hc["cc"]
    if any(l % 2 == 1 and k == "mix" for l, k in plan):
        shared["ident"] = hc["ident"]
    p = np.asarray(inputs["p"], np.float32)
    in_maps = []
    for ci, b in enumerate(cores):
        m = dict(shared)
        m["xT"] = xT_list[ci]
        m["pT"] = np.ascontiguousarray(p[:, b].transpose(0, 2, 1))
        in_maps.append(m)
    res = run_bass_kernel_spmd(nc, in_maps, core_ids=list(range(len(cores))))
    return [r["outT"] for r in res.results]


def kernel(**inputs):
    x = np.asarray(inputs["x"], np.float32)
    xT = [np.ascontiguousarray(x[b].T) for b in range(NCORES)]
    outs = run_plan(FULL_PLAN, inputs, xT, list(range(NCORES)))
    return np.stack([np.ascontiguousarray(o.T) for o in outs], axis=0).astype(np.float32)
```

```python
import numpy as np
import ml_dtypes
import concourse.bass as bass
import concourse.mybir as mybir
from concourse.bass_utils import run_bass_kernel_spmd
from contextlib import ExitStack

F32 = mybir.dt.float32
BF16 = mybir.dt.bfloat16
AF = mybir.ActivationFunctionType
ALU = mybir.AluOpType

D = 1024
S = 4096
DEPTH = 4
DFF = 2816
NFC = 22
PLE = 256
EPS = 1e-6
NCORES = 8

COMPUTE = ("pe", "act", "dve", "pool")
ENGS = ("pe", "act", "dve", "pool", "sp")
N_DMA_SEMS = 32


class T:
    __slots__ = ("name", "t", "w", "r")

    def __init__(self, name, t=None):
        self.name = name
        self.t = t
        self.w = []
        self.r = []

    def __getitem__(self, idx):
        return self.t[idx]


class Rot:
    def __init__(self, items):
        self.items = items
        self.i = 0

    def next(self):
        it = self.items[self.i % len(self.items)]
        self.i += 1
        return it


class Prog:
    def __init__(self, nc):
        self.nc = nc
        self.streams = {e: [] for e in ENGS}
        self.cnt = {e: 0 for e in COMPUTE}
        self.waited = {e: {} for e in ENGS}
        self.dma_i = 0
        self.es = ExitStack()

    def sbuf(self, name, shape, dtype):
        t = self.es.enter_context(self.nc.sbuf_tensor(name, list(shape), dtype))
        return T(name, t)

    def psum(self, name, shape, dtype=F32):
        t = self.es.enter_context(self.nc.psum_tensor(name, list(shape), dtype))
        return T(name, t)

    def dram(self, name, shape, dtype, kind="Internal"):
        t = self.nc.dram_tensor(name, list(shape), dtype, kind=kind)
        return T(name, t.ap())

    def _need(self, eng, ev, raw):
        semkey, val, src = ev
        if src == eng:
            if eng == "pe":
                return
            if not raw:
                return
        if self.waited[eng].get(semkey, 0) >= val:
            return
        self.waited[eng][semkey] = val
        self.streams[eng].append(("wait", semkey, val))

    def _deps(self, eng, reads, writes, appends):
        for t in reads:
            for ev in t.w:
                self._need(eng, ev, True)
        for t in writes:
            for ev in t.w:
                self._need(eng, ev, False)
            for ev in t.r:
                self._need(eng, ev, False)
        for t in appends:
            for ev in t.r:
                self._need(eng, ev, False)

    def _commit(self, ev, reads, writes, appends):
        semkey = ev[0]
        for t in reads:
            if not isinstance(semkey, tuple):
                t.r = [e for e in t.r if e[0] != semkey]
            t.r.append(ev)
        for t in writes:
            t.w = [ev]
            t.r = []
        for t in appends:
            if not isinstance(semkey, tuple):
                t.w = [e for e in t.w if e[0] != semkey]
            t.w.append(ev)
            t.r = []

    def op(self, eng, fn, reads=(), writes=(), appends=()):
        self._deps(eng, reads, writes, appends)
        self.cnt[eng] += 1
        ev = (eng, self.cnt[eng], eng)
        self.streams[eng].append(("op", fn, eng, 1))
        self._commit(ev, reads, writes, appends)
        return ev

    def mm(self, fns, reads=(), writes=(), appends=()):
        eng = "pe"
        self._deps(eng, reads, writes, appends)
        self.cnt[eng] += 1
        ev = (eng, self.cnt[eng], eng)
        for fn in fns[:-1]:
            self.streams[eng].append(("op", fn, None, 0))
        self.streams[eng].append(("op", fns[-1], eng, 1))
        self._commit(ev, reads, writes, appends)
        return ev

    def dma(self, q, fn, reads=(), writes=(), appends=()):
        self._deps(q, reads, writes, appends)
        i = self.dma_i
        self.dma_i += 1
        slot = i % N_DMA_SEMS
        k = i // N_DMA_SEMS
        semkey = ("dma", slot)
        if k > 0:
            self._need(q, (semkey, 16 * k, "dmaq"), True)
        ev = (semkey, 16 * (k + 1), "dmaq")
        self.streams[q].append(("op", fn, semkey, 16))
        self._commit(ev, reads, writes, appends)
        return ev

    def wait_all(self, eng, tiles):
        for t in tiles:
            for ev in t.w:
                self._need(eng, ev, True)

    def emit(self):
        nc = self.nc
        sems = {}
        for e in COMPUTE:
            sems[e] = self.es.enter_context(nc.semaphore("s_" + e))
        for s in range(N_DMA_SEMS):
            sems[("dma", s)] = self.es.enter_context(nc.semaphore("s_dma%d" % s))
        streams = self.streams

        def run(engobj, items):
            for it in items:
                if it[0] == "wait":
                    engobj.wait_ge(sems[it[1]], it[2])
                else:
                    _, fn, semkey, inc = it
                    ins = fn(engobj)
                    if inc:
                        ins.then_inc(sems[semkey], inc)

        with nc.Block() as block:
            @block.tensor
            def _(e):
                run(e, streams["pe"])

            @block.scalar
            def _(e):
                run(e, streams["act"])

            @block.vector
            def _(e):
                run(e, streams["dve"])

            @block.gpsimd
            def _(e):
                run(e, streams["pool"])

            @block.sync
            def _(e):
                run(e, streams["sp"])
        self.es.close()


VEC = {}
_off = 0
for _n, _k in [("mix_pre_g", 8), ("mix_post_g", 8), ("ffn_pre_g", 8), ("ffn_post_g", 8),
               ("ple_gate_g", 8), ("ple_b_g", 8), ("ffn_cw0", 44), ("ffn_cw1", 44),
               ("ffn_cw2", 44), ("ffn_cb", 44), ("ln_g", 4), ("ln_b", 4), ("conv_b", 4),
               ("sconv_w0", 4), ("sconv_w1", 4), ("sconv_w2", 4), ("conv_w", 124)]:
    VEC[_n] = _off
    _off += _k
NV = _off


def _pvec(v):
    return np.ascontiguousarray(np.asarray(v, np.float32).reshape(-1, 128).T)


def _chunk_layout(w):
    k, n = w.shape
    kc, oc = k // 128, n // 128
    return np.ascontiguousarray(
        np.asarray(w, np.float32).reshape(kc, 128, oc, 128).transpose(1, 2, 0, 3)).reshape(128, oc * kc * 128)


def weight_specs():
    specs = []
    for i in range(DEPTH):
        if i % 2 == 0:
            specs += [(f"wau{i}", 8 * 8 * 128), (f"wv{i}", 8 * 512), (f"wf{i}", 512), (f"wst{i}", 512),
                      (f"wo{i}", 8 * 8 * 128)]
        else:
            specs += [(f"win{i}", 20 * 8 * 128), (f"wo{i}", 8 * 8 * 128)]
        specs += [(f"wup{i}", 44 * 8 * 128), (f"wdn{i}", 8 * 22 * 128), (f"wg{i}", 8 * 8 * 128),
                  (f"wp{i}", 8 * 2 * 128)]
    return specs


class Ctx:
    pass


DEBUG_TILES = None
DEBUG_FFN = None


def tiles_for(halo, valid):
    out = []
    vs = 0
    while vs < S:
        wv = min(valid, S - vs)
        out.append((vs - halo, wv + 2 * halo, vs, wv))
        vs += wv
    if len(out) >= 2 and out[-1][3] < valid // 2:
        tot = out[-2][3] + out[-1][3]
        a = (tot // 2 + 1) // 2 * 2
        v0 = out[-2][2]
        out[-2:] = [(v0 - halo, a + 2 * halo, v0, a), (v0 + a - halo, tot - a + 2 * halo, v0 + a, tot - a)]
    if DEBUG_TILES is not None:
        out = [out[i] for i in DEBUG_TILES]
    return out


def xblocks(xb, a, b):
    a = max(a, 0)
    b = min(b, S)
    return [xb[i] for i in range(a // 512, (b - 1) // 512 + 1)]


class HT:
    def __init__(self, ap, deps):
        self.t = ap
        self.deps = deps

    def dep(self, c):
        return [self.deps[c]]

    def all(self):
        return list(dict.fromkeys(self.deps))


def load_x(P, C, src, s, W):
    xw = C.xw.next()
    a, b = max(s, 0), min(s + W, S)
    partial = (a > s or b < s + W)
    if a > s:
        P.op("pool", lambda e: e.memset(xw.t[:, :, 0:a - s], 0.0), writes=[xw])
    if b < s + W:
        P.op("pool", lambda e: e.memset(xw.t[:, :, b - s:W], 0.0), writes=[xw])
    sap = src["ap"].rearrange("(c p) t -> p c t", p=128)
    P.dma("sp", lambda e: e.dma_start(out=xw.t[:, :, a - s:b - s], in_=sap[:, :, a:b]),
          reads=xblocks(src["blk"], a, b), appends=[xw] if partial else (), writes=[] if partial else [xw])
    return xw


def store_x(P, C, dst, xw, off, vs, Wv):
    C.pending.append(lambda: _store_x(P, C, dst, xw, off, vs, Wv))


def _store_x(P, C, dst, xw, off, vs, Wv):
    dap = dst["ap"].rearrange("(c p) t -> p c t", p=128)
    P.dma("sp", lambda e: e.dma_start(out=dap[:, :, vs:vs + Wv], in_=xw.t[:, :, off:off + Wv]),
          reads=[xw], appends=xblocks(dst["blk"], vs, vs + Wv))


def pre_gen(P, C, job, src, s, W, gcol, prep=None):
    if prep is not None:
        prep()
    hT = job["hT"]
    xw = load_x(P, C, src, s, W)
    job["xw"] = xw
    st = C.ps_stat.next()

    def squares(c0):
        sqs = []
        for c in range(c0, c0 + 4):
            sq = C.sq.next()
            P.op("act", lambda e, c=c, sq=sq: e.activation(out=sq.t[:, :W], in_=xw.t[:, c, :W], func=AF.Square),
                 reads=[xw], writes=[sq])
            sqs.append(sq)
        return sqs

    def mms(c0, sqs):
        for i, sq in enumerate(sqs):
            c = c0 + i
            P.mm([lambda e, c=c, sq=sq: e.matmul(st.t[:, :W], lhsT=C.ones.t[:, :], rhs=sq.t[:, :W],
                                                 start=(c == 0), stop=(c == 7))],
                 reads=[sq, C.ones], writes=[st])
    s0 = squares(0)
    yield "A"
    mms(0, s0)
    yield "A"
    s1 = squares(4)
    yield "A"
    mms(4, s1)
    yield "B"
    rs = C.rs.next()
    P.op("act", lambda e: e.activation(out=rs.t[:, :W], in_=st.t[:, :W], func=AF.Sqrt, scale=1.0 / D, bias=C.epsb.t[:, 0:1]),
         reads=[st, C.epsb], writes=[rs])
    P.op("dve", lambda e: e.reciprocal(out=rs.t[:, :W], in_=rs.t[:, :W]), reads=[rs], writes=[rs])
    yield "B"
    for c in range(8):
        P.op("dve", lambda e, c=c: e.scalar_tensor_tensor(
            out=hT.t[:, c, :W], in0=xw.t[:, c, :W], scalar=C.vec.t[:, gcol + c:gcol + c + 1], in1=rs.t[:, :W],
            op0=ALU.mult, op1=ALU.mult), reads=[xw, rs, C.vec], writes=hT.dep(c))
    yield "E"


def wload(P, C, wb, col0, ncols):
    sl = C.wslots.next()
    first = True
    for a in range(0, ncols, 2048):
        b = min(ncols, a + 2048)
        P.dma("sp", lambda e, a=a, b=b: e.dma_start(out=sl.t[:, a:b], in_=wb.t[:, col0 + a:col0 + b]),
              reads=[wb], writes=[sl] if first else (), appends=() if first else [sl])
        first = False
    return sl


class PostNorm:
    def __init__(self, P, C, xw, off, Wv, gcol):
        self.P, self.C, self.xw, self.off, self.Wv, self.gcol = P, C, xw, off, Wv, gcol
        self.st = C.ps_stat.next()
        self.pend = None

    def _flush(self):
        if self.pend is not None:
            sq, first, last = self.pend
            st, Wv, C = self.st, self.Wv, self.C
            self.P.mm([lambda e: e.matmul(st.t[:, :Wv], lhsT=C.ones.t[:, :], rhs=sq.t[:, :Wv], start=first, stop=last)],
                      reads=[sq, C.ones], writes=[st])
            self.pend = None

    def chunk(self, dc, yb):
        P, C, Wv = self.P, self.C, self.Wv
        self._flush()
        sq = C.sq.next()
        P.op("act", lambda e: e.activation(out=sq.t[:, :Wv], in_=yb.t[:, :Wv], func=AF.Square), reads=[yb], writes=[sq])
        yg = C.yg[dc]
        g = self.gcol + dc
        P.op("act", lambda e: e.activation(out=yg.t[:, :Wv], in_=yb.t[:, :Wv], func=AF.Identity, scale=C.vec.t[:, g:g + 1]),
             reads=[yb, C.vec], writes=[yg])
        self.pend = (sq, dc == 0, dc == 7)

    def finish(self):
        P, C, Wv, xw, off = self.P, self.C, self.Wv, self.xw, self.off
        self._flush()
        st = self.st
        rs = C.rs.next()
        P.op("act", lambda e: e.activation(out=rs.t[:, :Wv], in_=st.t[:, :Wv], func=AF.Sqrt, scale=1.0 / D, bias=C.epsb.t[:, 0:1]),
             reads=[st, C.epsb], writes=[rs])
        P.op("dve", lambda e: e.reciprocal(out=rs.t[:, :Wv], in_=rs.t[:, :Wv]), reads=[rs], writes=[rs])
        for dc in range(8):
            yg = C.yg[dc]
            eng = "pool" if dc % 2 == 0 else "dve"
            P.op(eng, lambda e, yg=yg: e.tensor_tensor(out=yg.t[:, :Wv], in0=yg.t[:, :Wv], in1=rs.t[:, :Wv], op=ALU.mult),
                 reads=[rs, yg], writes=[yg])
            P.op("pool", lambda e, yg=yg, dc=dc: e.tensor_tensor(out=xw.t[:, dc, off:off + Wv], in0=xw.t[:, dc, off:off + Wv],
                                                                 in1=yg.t[:, :Wv], op=ALU.add),
                 reads=[yg], writes=[xw])


def out_proj(P, C, wo, mix, xw, off, vs, Wv, gcol, dst):
    pn = PostNorm(P, C, xw, off, Wv, gcol)
    for dc in range(8):
        wsl = wload(P, C, wo, dc * 1024, 1024)
        yb = C.ps.next()
        P.mm([lambda e, kc=kc, wsl=wsl, yb=yb: e.matmul(yb.t[:, :Wv], lhsT=wsl.t[:, kc * 128:(kc + 1) * 128], rhs=mix[kc].t[:, :Wv],
                                                       start=(kc == 0), stop=(kc == 7)) for kc in range(8)],
             reads=[wsl] + list(mix), writes=[yb])
        pn.chunk(dc, yb)
    pn.finish()
    store_x(P, C, dst, xw, off, vs, Wv)


def proj_fm(P, C, wb, oc, hT, W):
    wsl = wload(P, C, wb, oc * 1024, 1024)
    bk = C.ps.next()
    P.mm([lambda e, kc=kc: e.matmul(bk.t[:, :W], lhsT=wsl.t[:, kc * 128:(kc + 1) * 128], rhs=hT.t[:, kc, :W],
                                    start=(kc == 0), stop=(kc == 7)) for kc in range(8)],
         reads=[wsl] + hT.all(), writes=[bk])
    return bk


def sub_ple(P, C, li, src, dst):
    vb = li * NV
    wg, wp = C.wb[f"wg{li}"], C.wb[f"wp{li}"]
    jobs = []
    WG0, WP0 = 0, 16 * 512
    wg_blks = [C.big[i] for i in range(16)]
    wp_blks = [C.big[16 + i] for i in range(4)]

    def load_weights():
        for dc in range(8):
            for h in range(2):
                P.dma("sp", lambda e, dc=dc, h=h: e.dma_start(out=C.bigt[:, WG0 + dc * 1024 + h * 512:WG0 + dc * 1024 + (h + 1) * 512],
                                                             in_=wg.t[:, dc * 1024 + h * 512:dc * 1024 + (h + 1) * 512]),
                      reads=[wg], writes=[wg_blks[2 * dc + h]])
        for q in range(4):
            P.dma("sp", lambda e, q=q: e.dma_start(out=C.bigt[:, WP0 + q * 512:WP0 + (q + 1) * 512], in_=wp.t[:, q * 512:(q + 1) * 512]),
                  reads=[wp], writes=[wp_blks[q]])

    def mk(first, s, W, vs, Wv):
        def pre(job):
            def prep():
                pb = C.pb.next()
                job["pb"] = pb
                P.dma("sp", lambda e: e.dma_start(out=pb.t[:, :, :W], in_=C.pTb.t[li].rearrange("(c p) t -> p c t", p=128)[:, :, s:s + W]),
                      reads=[C.pTb_dep[li]], writes=[pb])
            return pre_gen(P, C, job, src, s, W, vb + VEC["ple_gate_g"], prep=prep)

        def body(job, inject):
            xw, hT, pb = job["xw"], job["hT"], job["pb"]
            if first:
                load_weights()
            for dc in range(8):
                gb = C.ps.next()
                P.mm([lambda e, kc=kc, gb=gb, dc=dc: e.matmul(
                    gb.t[:, :W], lhsT=C.bigt[:, WG0 + dc * 1024 + kc * 128:WG0 + dc * 1024 + (kc + 1) * 128], rhs=hT.t[:, kc, :W],
                    start=(kc == 0), stop=(kc == 7)) for kc in range(8)],
                    reads=[wg_blks[2 * dc], wg_blks[2 * dc + 1]] + hT.all(), writes=[gb])
                ppb = C.ps.next()
                P.mm([lambda e, kc=kc, ppb=ppb, dc=dc: e.matmul(
                    ppb.t[:, :W], lhsT=C.bigt[:, WP0 + dc * 256 + kc * 128:WP0 + dc * 256 + (kc + 1) * 128], rhs=pb.t[:, kc, :W],
                    start=(kc == 0), stop=(kc == 1)) for kc in range(2)],
                    reads=[wp_blks[dc // 2], pb], writes=[ppb])
                gt = C.f32.next()
                bcol = vb + VEC["ple_b_g"] + dc
                P.op("act", lambda e, gt=gt, gb=gb, bcol=bcol: e.activation(out=gt.t[:, :W], in_=gb.t[:, :W], func=AF.Sigmoid,
                                                                          bias=C.vec.t[:, bcol:bcol + 1]),
                     reads=[gb, C.vec], writes=[gt])
                P.op("dve", lambda e, gt=gt, ppb=ppb: e.tensor_tensor(out=gt.t[:, :W], in0=gt.t[:, :W], in1=ppb.t[:, :W], op=ALU.mult),
                     reads=[ppb, gt], writes=[gt])
                P.op("pool", lambda e, gt=gt, dc=dc: e.tensor_tensor(out=xw.t[:, dc, :W], in0=xw.t[:, dc, :W], in1=gt.t[:, :W], op=ALU.add),
                     reads=[gt], writes=[xw])
                if dc in (1, 2, 4):
                    inject(2)
            store_x(P, C, dst, xw, 0, vs, Wv)
        return {"kind": "ple", "pre": pre, "body": body}
    for i, tl_ in enumerate(tiles_for(0, 512)):
        jobs.append(mk(i == 0, *tl_))
    return jobs


def sub_ffn(P, C, li, src, dst):
    vb = li * NV
    wup, wdn = C.wb[f"wup{li}"], C.wb[f"wdn{li}"]
    jobs = []

    def mk(s, W, vs, Wv):
        def pre(job):
            return pre_gen(P, C, job, src, s, W, vb + VEC["ffn_pre_g"])

        def body(job, inject):
            xw, hT = job["xw"], job["hT"]
            acts = []
            pend2 = [None]
            for fc in range(NFC):
                wsl = wload(P, C, wup, fc * 2048, 2048)
                banks = []
                for h in range(2):
                    bk = C.ps.next()
                    P.mm([lambda e, kc=kc, h=h, wsl=wsl, bk=bk: e.matmul(
                        bk.t[:, :W], lhsT=wsl.t[:, h * 1024 + kc * 128:h * 1024 + (kc + 1) * 128], rhs=hT.t[:, kc, :W],
                        start=(kc == 0), stop=(kc == 7)) for kc in range(8)], reads=[wsl] + hT.all(), writes=[bk])
                    banks.append(bk)
                tmps = [C.f32.next(), C.f32.next()]
                cols = []
                for h in range(2):
                    ch = fc + h * NFC
                    cols.append((vb + VEC["ffn_cw0"] + ch, vb + VEC["ffn_cw1"] + ch, vb + VEC["ffn_cw2"] + ch, vb + VEC["ffn_cb"] + ch))
                for h in range(2):
                    bk, tt = banks[h], tmps[h]
                    c0, c1, c2, cb = cols[h]
                    P.op("act", lambda e, tt=tt, bk=bk, c1=c1, cb=cb: e.activation(
                        out=tt.t[:, :Wv], in_=bk.t[:, 1:1 + Wv], func=AF.Identity, scale=C.vec.t[:, c1:c1 + 1],
                        bias=C.vec.t[:, cb:cb + 1]), reads=[bk, C.vec], writes=[tt])
                for tap in (0, 2):
                    for h in range(2):
                        bk, tt = banks[h], tmps[h]
                        cc_ = cols[h][tap]
                        P.op("dve", lambda e, tt=tt, bk=bk, cc_=cc_, tap=tap: e.scalar_tensor_tensor(
                            out=tt.t[:, :Wv], in0=bk.t[:, tap:tap + Wv], scalar=C.vec.t[:, cc_:cc_ + 1], in1=tt.t[:, :Wv],
                            op0=ALU.mult, op1=ALU.add), reads=[bk, C.vec, tt], writes=[tt])
                def stage2(fc=fc, tmps=tmps):
                    P.op("act", lambda e, tt=tmps[0]: e.activation(out=tt.t[:, :Wv], in_=tt.t[:, :Wv], func=AF.Gelu),
                         reads=[tmps[0]], writes=[tmps[0]])
                    ab = C.big[fc]
                    P.op("pool", lambda e, fc=fc, a=tmps[0], b=tmps[1]: e.tensor_tensor(
                        out=C.bigt[:, fc * 512:fc * 512 + Wv], in0=a.t[:, :Wv], in1=b.t[:, :Wv], op=ALU.mult),
                        reads=[tmps[0], tmps[1]], writes=[ab])
                    acts.append(ab)
                if pend2[0] is not None:
                    pend2[0]()
                pend2[0] = stage2
                if fc == 10:
                    inject(0)
                elif fc in (13, 15, 16, 18):
                    inject(1)
            pend2[0]()
            pn = PostNorm(P, C, xw, 1, Wv, vb + VEC["ffn_post_g"])
            for dc in range(8):
                HF = NFC // 2
                wh = [wload(P, C, wdn, dc * NFC * 128, HF * 128), wload(P, C, wdn, dc * NFC * 128 + HF * 128, HF * 128)]
                yb = C.ps.next()
                P.mm([lambda e, fc=fc, wh=wh, yb=yb: e.matmul(
                    yb.t[:, :Wv], lhsT=wh[fc // HF].t[:, (fc % HF) * 128:(fc % HF + 1) * 128], rhs=C.bigt[:, fc * 512:fc * 512 + Wv],
                    start=(fc == 0), stop=(fc == NFC - 1)) for fc in range(NFC)], reads=wh + acts, writes=[yb])
                pn.chunk(dc, yb)
                if dc == 0:
                    inject(2)
            pn.finish()
            store_x(P, C, dst, xw, 1, vs, Wv)
        return {"kind": "ffn", "pre": pre, "body": body}
    for tl_ in tiles_for(1, 510):
        jobs.append(mk(*tl_))
    return jobs


def prep_odd(P, C, li):
    vb = li * NV
    for k in range(31):
        for g in range(4):
            idx = k * 4 + g
            col = vb + VEC["conv_w"] + idx
            blk = C.big[32 + idx // 4]
            P.op("pool", lambda e, idx=idx, col=col: e.tensor_scalar(
                out=C.bigt[:, 16384 + idx * 128:16384 + (idx + 1) * 128], in0=C.ident.t[:, :],
                scalar1=C.vec.t[:, col:col + 1], scalar2=0.0, op0=ALU.mult, op1=ALU.add),
                reads=[C.ident, C.vec], appends=[blk])


def sub_odd(P, C, li, src, dst):
    vb = li * NV
    win, wo = C.wb[f"win{li}"], C.wb[f"wo{li}"]
    diag_blks = [C.big[32 + i] for i in range(31)]
    H = 16
    jobs = []

    def mk(first, s, W, vs, Wv):
        def pre(job):
            return pre_gen(P, C, job, src, s, W, vb + VEC["mix_pre_g"], prep=(lambda: prep_odd(P, C, li)) if first else None)

        def body(job, inject):
            xw, hT = job["xw"], job["hT"]
            mix = [None] * 8
            st = [dict() for _ in range(4)]

            def S1(g):
                d = st[g]
                ab = proj_fm(P, C, win, g, hT, W)
                gb = proj_fm(P, C, win, 4 + g, hT, W)
                sg = C.f32.next()
                P.op("act", lambda e: e.activation(out=sg.t[:, :W], in_=gb.t[:, :W], func=AF.Sigmoid), reads=[gb], writes=[sg])
                gl = C.b16.next()
                P.op("dve", lambda e: e.tensor_tensor(out=gl.t[:, :W], in0=ab.t[:, :W], in1=sg.t[:, :W], op=ALU.mult),
                     reads=[ab, sg], writes=[gl])
                d.update(sg=sg, gl=gl)

            def S2(g):
                d = st[g]
                cgb = proj_fm(P, C, win, 12 + g, hT, W)
                xb_ = proj_fm(P, C, win, 16 + g, hT, W)
                xs = C.f32.next()
                P.op("act", lambda e: e.activation(out=xs.t[:, :W], in_=xb_.t[:, :W], func=AF.Copy), reads=[xb_], writes=[xs])
                P.op("dve", lambda e: e.tensor_tensor(out=xs.t[:, :W], in0=cgb.t[:, :W], in1=xs.t[:, :W], op=ALU.mult),
                     reads=[cgb, xs], writes=[xs])
                d.update(xs=xs)

            def S3(g):
                d = st[g]
                gl = d["gl"]
                cbk = C.ps.next()
                P.mm([lambda e, k=k: e.matmul(
                    cbk.t[:, :Wv], lhsT=C.bigt[:, 16384 + (k * 4 + g) * 128:16384 + (k * 4 + g + 1) * 128],
                    rhs=gl.t[:, H - 15 + k:H - 15 + k + Wv], start=(k == 0), stop=(k == 30)) for k in range(31)],
                    reads=[gl] + diag_blks, writes=[cbk])
                if g >= 1:
                    inject(2)
                cb = C.f32.next()
                bcol = vb + VEC["conv_b"] + g
                P.op("act", lambda e: e.activation(out=cb.t[:, :Wv], in_=cbk.t[:, :Wv], func=AF.Identity, bias=C.vec.t[:, bcol:bcol + 1]),
                     reads=[cbk, C.vec], writes=[cb])
                c16 = C.b16.next()
                P.op("act", lambda e: e.activation(out=c16.t[:, :Wv], in_=cbk.t[:, :Wv], func=AF.Identity, bias=C.vec.t[:, bcol:bcol + 1]),
                     reads=[cbk, C.vec], writes=[c16])
                d.update(cb=cb, c16=c16)

            def S4(g):
                d = st[g]
                cb, c16, xs = d["cb"], d["c16"], d["xs"]
                mb = C.ps.next()
                P.mm([lambda e: e.matmul(mb.t[:, :Wv], lhsT=C.onesdiv.t[:, :], rhs=c16.t[:, :Wv], start=True, stop=True)],
                     reads=[c16, C.onesdiv], writes=[mb])
                bb = proj_fm(P, C, win, 8 + g, hT, W)
                P.op("dve", lambda e: e.tensor_tensor(out=cb.t[:, :Wv], in0=cb.t[:, :Wv], in1=mb.t[:, :Wv], op=ALU.subtract),
                     reads=[mb, cb], writes=[cb])
                dsq = C.b16.next()
                P.op("act", lambda e: e.activation(out=dsq.t[:, :Wv], in_=cb.t[:, :Wv], func=AF.Square), reads=[cb], writes=[dsq])
                tt = C.f32.next()
                w0 = vb + VEC["sconv_w0"] + g
                P.op("dve", lambda e: e.tensor_scalar(out=tt.t[:, :Wv], in0=xs.t[:, H - 1:H - 1 + Wv],
                                                      scalar1=C.vec.t[:, w0:w0 + 1], scalar2=0.0, op0=ALU.mult, op1=ALU.add),
                     reads=[xs, C.vec], writes=[tt])
                d.update(bb=bb, dsq=dsq, tt=tt)

            def S5(g):
                d = st[g]
                sd, cb, xs, tt, bb, dsq = d["sg"], d["cb"], d["xs"], d["tt"], d["bb"], d["dsq"]
                w1, w2 = vb + VEC["sconv_w1"] + g, vb + VEC["sconv_w2"] + g
                vbk = C.ps.next()
                P.mm([lambda e: e.matmul(vbk.t[:, :Wv], lhsT=C.onesdiv.t[:, :], rhs=dsq.t[:, :Wv], start=True, stop=True)],
                     reads=[dsq, C.onesdiv], writes=[vbk])
                P.op("act", lambda e: e.activation(out=sd.t[:, :Wv], in_=vbk.t[:, :Wv], func=AF.Sqrt, bias=C.epsb.t[:, 0:1]),
                     reads=[vbk, C.epsb], writes=[sd])
                P.op("dve", lambda e: e.scalar_tensor_tensor(
                    out=tt.t[:, :Wv], in0=xs.t[:, H:H + Wv], scalar=C.vec.t[:, w1:w1 + 1], in1=tt.t[:, :Wv],
                    op0=ALU.mult, op1=ALU.add), reads=[xs, C.vec, tt], writes=[tt])
                P.op("dve", lambda e: e.reciprocal(out=sd.t[:, :Wv], in_=sd.t[:, :Wv]), reads=[sd], writes=[sd])
                P.op("dve", lambda e: e.scalar_tensor_tensor(
                    out=tt.t[:, :Wv], in0=xs.t[:, H + 1:H + 1 + Wv], scalar=C.vec.t[:, w2:w2 + 1], in1=tt.t[:, :Wv],
                    op0=ALU.mult, op1=ALU.add), reads=[xs, C.vec, tt], writes=[tt])
                P.op("dve", lambda e: e.tensor_tensor(out=cb.t[:, :Wv], in0=cb.t[:, :Wv], in1=sd.t[:, :Wv], op=ALU.mult),
                     reads=[sd, cb], writes=[cb])
                mx = C.mix[g]
                gcol, bcol2 = vb + VEC["ln_g"] + g, vb + VEC["ln_b"] + g
                P.op("act", lambda e: e.activation(out=mx.t[:, :Wv], in_=cb.t[:, :Wv], func=AF.Silu, scale=C.vec.t[:, gcol:gcol + 1],
                                                   bias=C.vec.t[:, bcol2:bcol2 + 1]), reads=[cb, C.vec], writes=[mx])
                mix[g] = mx
                mx2 = C.mix[4 + g]
                P.op("dve", lambda e: e.tensor_tensor(out=mx2.t[:, :Wv], in0=bb.t[:, H:H + Wv], in1=tt.t[:, :Wv], op=ALU.mult),
                     reads=[bb, tt], writes=[mx2])
                mix[4 + g] = mx2

            S1(0)
            S2(0)
            S3(0)
            for g in range(1, 4):
                S1(g)
                S4(g - 1)
                S2(g)
                S5(g - 1)
                S3(g)
            S4(3)
            S5(3)
            out_proj(P, C, wo, mix, xw, H, vs, Wv, vb + VEC["mix_post_g"], dst)
        return {"kind": "odd", "pre": pre, "body": body}
    for i, tl_ in enumerate(tiles_for(H, 480)):
        jobs.append(mk(i == 0, *tl_))
    return jobs


def prep_even(P, C, li):
    vb = li * NV
    wv, wf, wst = C.wb[f"wv{li}"], C.wb[f"wf{li}"], C.wb[f"wst{li}"]
    bs = C.bs[li]
    wfs = wload(P, C, wf, 0, 512)
    wcs = C.wcs
    for g in range(4):
        bk = C.ps.next()
        P.mm([lambda e, g=g, bk=bk: e.matmul(bk.t[:, 0:128], lhsT=C.cc.t[:, 0:128], rhs=wfs.t[:, g * 128:(g + 1) * 128],
                                             start=True, stop=True),
              lambda e, g=g, bk=bk: e.matmul(bk.t[:, 128:256], lhsT=C.cc.t[:, 128:256], rhs=wfs.t[:, g * 128:(g + 1) * 128],
                                             start=True, stop=True)], reads=[wfs, C.cc], writes=[bk])
        P.op("act", lambda e, g=g, bk=bk: e.activation(out=wcs.t[:, g * 256:(g + 1) * 256], in_=bk.t[:, 0:256], func=AF.Copy),
             reads=[bk], appends=[wcs])
    wsts = C.wsts
    P.dma("sp", lambda e: e.dma_start(out=wsts.t[:, :], in_=wst.t[:, :]), reads=[wst], writes=[wsts])
    wst32 = C.f32.next()
    P.dma("sp", lambda e: e.dma_start(out=wst32.t[:, :], in_=C.win32[f"wst{li}"].t[:, :]), reads=[], writes=[wst32])
    bs0 = C.f32.next()
    P.op("pool", lambda e: e.memset(bs0.t[:, :], 0.0), writes=[bs0])
    P.dma("sp", lambda e: e.dma_start(out=bs0.t[0:1, :], in_=bs.t[0:1, :]), reads=[], appends=[bs0])
    bias2 = C.bias2
    for g in range(4):
        bk = C.ps.next()
        P.mm([lambda e, g=g, bk=bk: e.matmul(bk.t[:, 0:128], lhsT=C.ones32.t[:, :], rhs=wst32.t[:, g * 128:(g + 1) * 128],
                                             start=True, stop=True),
              lambda e, g=g, bk=bk: e.matmul(bk.t[:, 128:256], lhsT=C.ones32.t[:, :], rhs=bs0.t[:, g * 128:(g + 1) * 128],
                                             start=True, stop=True)], reads=[wst32, bs0, C.ones32], writes=[bk])
        lb = vb + VEC["ln_b"] + g
        tmp = C.f32.next()
        P.op("act", lambda e, bk=bk, tmp=tmp: e.activation(out=tmp.t[:, 0:128], in_=bk.t[:, 128:256], func=AF.Copy),
             reads=[bk], writes=[tmp])
        P.op("dve", lambda e, g=g, bk=bk, lb=lb, tmp=tmp: e.scalar_tensor_tensor(
            out=bias2.t[:, g * 128:(g + 1) * 128], in0=bk.t[:, 0:128], scalar=C.vec.t[:, lb:lb + 1], in1=tmp.t[:, 0:128],
            op0=ALU.mult, op1=ALU.add), reads=[bk, tmp, C.vec], appends=[bias2])


def sub_even(P, C, li, src, dst):
    vb = li * NV
    wau, wo = C.wb[f"wau{li}"], C.wb[f"wo{li}"]
    wcs, wsts, bias2 = C.wcs, C.wsts, C.bias2
    wv = C.wb[f"wv{li}"]
    tl = tiles_for(0, 512)
    jobs = []

    def mk1(ti, s, W, vs, Wv):
        def pre(job):
            return pre_gen(P, C, job, src, s, W, vb + VEC["mix_pre_g"], prep=(lambda: prep_even(P, C, li)) if ti == 0 else None)

        def body(job, inject):
            hT = job["hT"]
            ats = []
            for g in range(4):
                bk = proj_fm(P, C, wau, g, hT, W)
                at = C.b16.next()
                P.op("act", lambda e, at=at, bk=bk: e.activation(out=at.t[:, :W], in_=bk.t[:, :W], func=AF.Copy), reads=[bk], writes=[at])
                ats.append(at)
            inject(2)
            for ts_ in range(4):
                sc = ti * 4 + ts_
                for half in range(2):
                    bk = C.ps.next()
                    P.mm([lambda e, g=g, ts_=ts_, bk=bk, j=j: e.matmul(
                        bk.t[:, j * 256:(j + 1) * 256], lhsT=ats[g].t[:, ts_ * 128:(ts_ + 1) * 128], rhs=wcs.t[:, g * 256:(g + 1) * 256],
                        start=True, stop=True) for j, g in enumerate((2 * half, 2 * half + 1))],
                        reads=ats + [wcs], writes=[bk])
                    blk = C.big[2 * sc + half]
                    if half == 0:
                        P.op("act", lambda e, sc=sc, half=half, bk=bk: e.activation(
                            out=C.bigt[:, sc * 1024 + half * 512:sc * 1024 + (half + 1) * 512], in_=bk.t[:, :], func=AF.Copy),
                            reads=[bk], writes=[blk])
                    else:
                        P.op("dve", lambda e, sc=sc, half=half, bk=bk: e.tensor_copy(
                            out=C.bigt[:, sc * 1024 + half * 512:sc * 1024 + (half + 1) * 512], in_=bk.t[:, :]),
                            reads=[bk], writes=[blk])
                if ts_ in (0, 2):
                    inject(2)
        return {"kind": "even", "pre": pre, "body": body}
    for ti, tl_ in enumerate(tl):
        jobs.append(mk1(ti, *tl_))

    allB = [C.big[i] for i in range(64)]

    def mk2(ti, s, W, vs, Wv):
        def pre(job):
            return pre_gen(P, C, job, src, s, W, vb + VEC["mix_pre_g"])

        def body(job, inject):
            xw, hT = job["xw"], job["hT"]
            vns = []
            wvh = [wload(P, C, wv, 0, 2048), wload(P, C, wv, 2048, 2048)]
            for ts_ in range(4):
                bk = C.ps.next()
                P.mm([lambda e, kc=kc, ts_=ts_, bk=bk: e.matmul(bk.t[:, :], lhsT=hT.t[:, kc, ts_ * 128:(ts_ + 1) * 128],
                                                               rhs=wvh[kc // 4].t[:, (kc % 4) * 512:(kc % 4 + 1) * 512],
                                                               start=(kc == 0), stop=(kc == 7))
                      for kc in range(8)], reads=hT.all() + wvh, writes=[bk])
                vg = C.f32.next()
                P.op("act", lambda e, vg=vg, bk=bk: e.activation(out=vg.t[:, :], in_=bk.t[:, :], func=AF.Gelu), reads=[bk], writes=[vg])
                stt = C.stt.next()
                for g in range(4):
                    P.op("dve", lambda e, g=g, vg=vg, stt=stt: e.bn_stats(out=stt.t[:, g * 6:(g + 1) * 6], in_=vg.t[:, g * 128:(g + 1) * 128]),
                         reads=[vg], appends=[stt])
                mv = C.mv.next()
                for g in range(4):
                    P.op("dve", lambda e, g=g, mv=mv, stt=stt: e.bn_aggr(out=mv.t[:, g * 2:(g + 1) * 2], in_=stt.t[:, g * 6:(g + 1) * 6]),
                         reads=[stt], appends=[mv])
                rr = C.rr.next()
                P.op("act", lambda e, rr=rr, mv=mv: e.activation(out=rr.t[:, 0:4], in_=mv.t[:, 1:8:2], func=AF.Sqrt, bias=C.epsb.t[:, 0:1]),
                     reads=[mv, C.epsb], writes=[rr])
                P.op("dve", lambda e, rr=rr: e.reciprocal(out=rr.t[:, 0:4], in_=rr.t[:, 0:4]), reads=[rr], writes=[rr])
                vn = C.b16.next()
                for g in range(4):
                    P.op("dve", lambda e, g=g, vn=vn, vg=vg, mv=mv, rr=rr: e.tensor_scalar(
                        out=vn.t[:, g * 128:(g + 1) * 128], in0=vg.t[:, g * 128:(g + 1) * 128], scalar1=mv.t[:, 2 * g:2 * g + 1],
                        scalar2=rr.t[:, g:g + 1], op0=ALU.subtract, op1=ALU.mult), reads=[vg, mv, rr], appends=[vn])
                vns.append(vn)
            uts = []
            for g in range(4):
                bk = proj_fm(P, C, wau, 4 + g, hT, W)
                ut = C.ut[g]
                P.op("act", lambda e, ut=ut, bk=bk: e.activation(out=ut.t[:, :W], in_=bk.t[:, :W], func=AF.Gelu), reads=[bk], writes=[ut])
                uts.append(ut)
            for ts_ in range(4):
                vn = vns[ts_]
                mbk = C.ps.next()
                P.mm([lambda e, g=g, vn=vn, mbk=mbk: e.matmul(mbk.t[:, g * 128:(g + 1) * 128], lhsT=vn.t[:, g * 128:(g + 1) * 128],
                                                             rhs=wsts.t[:, g * 128:(g + 1) * 128], start=True, stop=True) for g in range(4)],
                     reads=[vn, wsts], writes=[mbk])
                for g in range(4):
                    tt = C.f32.next()
                    lg = vb + VEC["ln_g"] + g
                    P.op("dve", lambda e, g=g, tt=tt, mbk=mbk, lg=lg: e.scalar_tensor_tensor(
                        out=tt.t[:, 0:128], in0=mbk.t[:, g * 128:(g + 1) * 128], scalar=C.vec.t[:, lg:lg + 1],
                        in1=bias2.t[:, g * 128:(g + 1) * 128], op0=ALU.mult, op1=ALU.add), reads=[mbk, bias2, C.vec], writes=[tt])
                    mx = C.mix[4 + g]
                    P.op("pool", lambda e, g=g, tt=tt, mx=mx, ts_=ts_: e.tensor_tensor(
                        out=mx.t[:, ts_ * 128:(ts_ + 1) * 128], in0=tt.t[:, 0:128], in1=uts[g].t[:, ts_ * 128:(ts_ + 1) * 128],
                        op=ALU.mult), reads=[tt, uts[g]], appends=[mx])
            fb = [C.ps.next() for _ in range(4)]
            for sb in range(8):
                col0 = ti * 16384 + sb * 2048
                csl = wload(P, C, C.dftc, col0, 2048)
                ssl = wload(P, C, C.dfts, col0, 2048)
                fns = []
                for g in range(4):
                    for q in range(4):
                        sc = sb * 4 + q
                        fns.append(lambda e, g=g, q=q, sc=sc, csl=csl: e.matmul(
                            fb[g].t[:, :], lhsT=C.bigt[:, sc * 1024 + g * 256:sc * 1024 + g * 256 + 128], rhs=csl.t[:, q * 512:(q + 1) * 512],
                            start=(sc == 0), stop=False))
                        fns.append(lambda e, g=g, q=q, sc=sc, ssl=ssl: e.matmul(
                            fb[g].t[:, :], lhsT=C.bigt[:, sc * 1024 + g * 256 + 128:sc * 1024 + g * 256 + 256], rhs=ssl.t[:, q * 512:(q + 1) * 512],
                            start=False, stop=(sc == 31)))
                P.mm(fns, reads=[csl, ssl] + allB, writes=fb)
                if sb in (1, 3, 5):
                    inject(2)
            mix = []
            for g in range(4):
                mx = C.mix[g]
                P.op("act", lambda e, g=g, mx=mx: e.activation(out=mx.t[:, :], in_=fb[g].t[:, :], func=AF.Copy), reads=[fb[g]], writes=[mx])
                mix.append(mx)
            mix += [C.mix[4 + g] for g in range(4)]
            out_proj(P, C, wo, mix, xw, 0, vs, Wv, vb + VEC["mix_post_g"], dst)
        return {"kind": "even", "pre": pre, "body": body}
    for ti, tl_ in enumerate(tl):
        jobs.append(mk2(ti, *tl_))
    return jobs


def build_program(plan, need_dft=True):
    nc = bass.Bass("TRN2", target_bir_lowering=False)
    P = Prog(nc)
    C = Ctx()
    xin = P.dram("xT", [D, S], F32, kind="ExternalInput")
    C.pT = P.dram("pT", [DEPTH, PLE, S], F32, kind="ExternalInput")
    vecs = P.dram("vecs", [128, DEPTH * NV], F32, kind="ExternalInput")
    out = P.dram("outT", [D, S], F32, kind="ExternalOutput")
    layers = sorted(set(l for l, _ in plan))
    kinds = set(plan)
    C.win32, C.wb, C.bs = {}, {}, {}
    used = []

    def wkind(name):
        return "mix" if name[:-1] in ("wau", "wv", "wf", "wst", "wo", "win") else ("ffn" if name[:-1] in ("wup", "wdn") else "ple")
    for name, ncols in weight_specs():
        li = int(name[-1])
        if (li, wkind(name)) not in kinds:
            continue
        C.win32[name] = P.dram(name, [128, ncols], F32, kind="ExternalInput")
        C.wb[name] = P.dram(name + "_b", [128, ncols], BF16)
        used.append((name, ncols))
    for li in layers:
        if li % 2 == 0 and (li, "mix") in kinds:
            C.bs[li] = P.dram(f"bs{li}", [1, 512], F32, kind="ExternalInput")
    has_even = any(l % 2 == 0 and k == "mix" for l, k in plan)
    has_odd = any(l % 2 == 1 and k == "mix" for l, k in plan)
    if has_even:
        C.dftc = P.dram("dftc", [128, 8 * 32 * 512], BF16, kind="ExternalInput")
        C.dfts = P.dram("dfts", [128, 8 * 32 * 512], BF16, kind="ExternalInput")
        ccd = P.dram("cc", [128, 256], BF16, kind="ExternalInput")
    if has_odd:
        identd = P.dram("ident", [128, 128], BF16, kind="ExternalInput")
    xa = P.dram("xa", [D, S], F32)
    xb = P.dram("xb", [D, S], F32)
    C.pTb = P.dram("pTb", [DEPTH, PLE, S], BF16)
    C.pTb_dep = [T("pTb%d" % i) for i in range(DEPTH)]

    def xbuf(t):
        return {"ap": t.t, "blk": [T(t.name + "_b%d" % i) for i in range(8)]}
    bufs = [xbuf(xin)] + [None] * (len(plan) - 1)
    xab = [xbuf(xa), xbuf(xb)]
    for j in range(1, len(plan)):
        bufs[j] = xab[j % 2]
    bufs.append(xbuf(out))

    C.vec = P.sbuf("vec", [128, DEPTH * NV], F32)
    C.ones = P.sbuf("ones", [128, 128], BF16)
    C.onesdiv = P.sbuf("onesdiv", [128, 128], BF16)
    C.epsb = P.sbuf("epsb", [128, 1], F32)
    C.xw = Rot([P.sbuf("xw%d" % i, [128, 8, 512], F32) for i in range(2)])
    hT0 = P.sbuf("hT0", [128, 8, 512], BF16)
    C.sq = Rot([P.sbuf("sq%d" % i, [128, 512], BF16) for i in range(4)])
    C.rs = Rot([P.sbuf("rs%d" % i, [128, 512], F32) for i in range(2)])
    C.yg = [P.sbuf("yg%d" % i, [128, 512], F32) for i in range(8)]
    C.wslots = Rot([P.sbuf("wsl%d" % i, [128, 2048], BF16) for i in range(6)])
    bigt = P.sbuf("big", [128, 32768], BF16)
    C.bigt = bigt.t
    C.big = [T("big%d" % i) for i in range(64)]
    hts = [HT(hT0.t, [hT0] * 8),
           HT(C.bigt[:, 12288:16384].rearrange("p (c w) -> p c w", c=8), [C.big[24 + c] for c in range(8)])]
    C.f32 = Rot([P.sbuf("f32_%d" % i, [128, 512], F32) for i in range(8)])
    C.b16 = Rot([P.sbuf("b16_%d" % i, [128, 512], BF16) for i in range(8)])
    C.mix = [P.sbuf("mix%d" % i, [128, 512], BF16) for i in range(8)]
    C.pb = Rot([P.sbuf("pb%d" % i, [128, 2, 512], BF16) for i in range(2)])
    if has_even:
        C.cc = P.sbuf("cc_s", [128, 256], BF16)
        C.wcs = P.sbuf("wcs", [128, 1024], BF16)
        C.wsts = P.sbuf("wsts", [128, 512], BF16)
        C.bias2 = P.sbuf("bias2", [128, 512], F32)
        C.ones32 = P.sbuf("ones32", [128, 128], F32)
        C.ut = [P.sbuf("ut%d" % i, [128, 512], BF16) for i in range(4)]
        C.stt = Rot([P.sbuf("stt%d" % i, [128, 24], F32) for i in range(4)])
        C.mv = Rot([P.sbuf("mv%d" % i, [128, 8], F32) for i in range(4)])
        C.rr = Rot([P.sbuf("rr%d" % i, [128, 4], F32) for i in range(4)])
    if has_odd:
        C.ident = P.sbuf("ident_s", [128, 128], BF16)
    C.ps = Rot([P.psum("ps%d" % i, [128, 512]) for i in range(6)])
    C.ps_stat = Rot([P.psum("pst%d" % i, [128, 512]) for i in range(2)])

    P.op("pool", lambda e: e.memset(C.ones.t[:, :], 1.0), writes=[C.ones])
    P.op("pool", lambda e: e.memset(C.onesdiv.t[:, :], 1.0 / 128), writes=[C.onesdiv])
    P.op("pool", lambda e: e.memset(C.epsb.t[:, :], EPS), writes=[C.epsb])
    P.dma("sp", lambda e: e.dma_start(out=C.vec.t[:, :], in_=vecs.t[:, :]), writes=[C.vec])
    if has_even:
        P.op("pool", lambda e: e.memset(C.ones32.t[:, :], 1.0), writes=[C.ones32])
        P.dma("sp", lambda e: e.dma_start(out=C.cc.t[:, :], in_=ccd.t[:, :]), writes=[C.cc])
    if has_odd:
        P.dma("sp", lambda e: e.dma_start(out=C.ident.t[:, :], in_=identd.t[:, :]), writes=[C.ident])

    def casts_for(k):
        li, kind = plan[k]
        todo = []
        for name, ncols in used:
            if int(name[-1]) == li and wkind(name) == kind:
                if name.startswith("wg"):
                    for h in range(2):
                        todo.append(lambda li=li, h=h: P.dma(
                            "pool", lambda e: e.dma_start(out=C.pTb.t[li, :, h * 2048:(h + 1) * 2048], in_=C.pT.t[li, :, h * 2048:(h + 1) * 2048]),
                            appends=[C.pTb_dep[li]]))
                step = 8192
                for c0 in range(0, ncols, step):
                    c1 = min(ncols, c0 + step)
                    todo.append(lambda name=name, c0=c0, c1=c1: P.dma(
                        "pool", lambda e: e.dma_start(out=C.wb[name].t[:, c0:c1], in_=C.win32[name].t[:, c0:c1]),
                        appends=[C.wb[name]]))
        return todo
    carry = {}
    for j, (li, kind) in enumerate(plan):
        ks = []
        if j == 0:
            ks = [k for k in range(1, len(plan)) if plan[k][0] == li]
        elif kind == "ffn":
            ks = [k for k in range(j + 1, len(plan)) if plan[k][0] == li + 1 and plan[k][1] != "ffn"]
        elif kind == "mix":
            ks = [k for k in range(j + 1, len(plan)) if plan[k][0] == li and plan[k][1] == "ffn"]
        carry[j] = ks
    carried = set(k for ks in carry.values() for k in ks)
    for k in range(len(plan)):
        if k == 0 or k not in carried:
            for f in casts_for(k):
                f()

    jobs = []
    for j, (li, kind) in enumerate(plan):
        src, dst = bufs[j], bufs[j + 1]
        n0 = len(jobs)
        if kind == "ple":
            jobs += sub_ple(P, C, li, src, dst)
        elif kind == "ffn":
            jobs += sub_ffn(P, C, li, src, dst)
        elif li % 2 == 1:
            jobs += sub_odd(P, C, li, src, dst)
        else:
            jobs += sub_even(P, C, li, src, dst)
        todo = []
        for k in carry[j]:
            todo += casts_for(k)
        njobs = len(jobs) - n0
        for q, f in enumerate(todo):
            jobs[n0 + min(q * njobs // max(len(todo), 1), njobs - 1)].setdefault("casts", []).append(f)
    n = len(jobs)
    i = 0
    while i < n:
        if jobs[i]["kind"] == "even":
            jobs[i]["hT"] = hts[0]
            i += 1
            continue
        e = i
        while e < n and jobs[e]["kind"] != "even":
            e += 1
        for q in range(i, e):
            jobs[q]["hT"] = hts[(q - i) % 2]
        i = e

    def exhaust(g):
        if g is not None:
            for _ in g:
                pass
    C.pending = []
    g0 = jobs[0]["pre"](jobs[0])
    exhaust(g0)
    for j in range(n):
        g = jobs[j + 1]["pre"](jobs[j + 1]) if j + 1 < n else None

        late_b = (j + 1 < n and jobs[j + 1]["hT"] is jobs[j]["hT"] and jobs[j]["kind"] != "even")
        state = {"tag": "A", "done": g is None}

        def inject(k=1, g=g, late_b=late_b, state=state):
            for st_ in pend_prev:
                st_()
            del pend_prev[:]
            for _ in range(k):
                if state["done"] or (late_b and state["tag"] == "B"):
                    return
                try:
                    state["tag"] = next(g)
                except StopIteration:
                    state["done"] = True
        pend_prev = C.pending
        C.pending = []
        for f in jobs[j].get("casts", ()):
            f()
        jobs[j]["body"](jobs[j], inject)
        for st_ in pend_prev:
            st_()
        exhaust(g)
    for st_ in C.pending:
        st_()
    P.wait_all("sp", bufs[-1]["blk"])
    P.emit()
    return nc


def host_consts():
    bf = ml_dtypes.bfloat16
    out = {}
    s = np.arange(S, dtype=np.int64)
    idx = (s[:, None] * s[None, :]) % S
    ang = 2.0 * np.pi * np.arange(S, dtype=np.float64) / S
    ct = np.cos(ang).astype(np.float32)
    st = (-np.sin(ang)).astype(np.float32)

    def lay(tab):
        m = tab[idx]
        m = m.reshape(32, 128, 8, 512).transpose(1, 2, 0, 3)
        return np.ascontiguousarray(m).reshape(128, 8 * 32 * 512).astype(bf)
    out["dftc"] = lay(ct)
    out["dfts"] = lay(st)
    c = np.arange(128, dtype=np.float64)
    a = 2.0 * np.pi * np.outer(c, c) / 128
    sc = 1.0 / np.sqrt(float(S) * 128.0)
    out["cc"] = np.concatenate([np.cos(a) * sc, np.sin(a) * sc], axis=1).astype(np.float32).astype(bf)
    out["ident"] = np.eye(128, dtype=np.float32).astype(bf)
    return out


def host_weights(inp):
    w = {}
    vecs = np.zeros((128, DEPTH * NV), np.float32)
    for i in range(DEPTH):
        j = i // 2
        vb = i * NV

        def put(name, arr):
            vecs[:, vb + VEC[name]:vb + VEC[name] + arr.shape[1]] = arr
        put("mix_pre_g", _pvec(inp["mix_pre_g"][i]))
        put("mix_post_g", _pvec(inp["mix_post_g"][i]))
        put("ffn_pre_g", _pvec(inp["ffn_pre_g"][i]))
        put("ffn_post_g", _pvec(inp["ffn_post_g"][i]))
        put("ple_gate_g", _pvec(inp["ple_gate_g"][i]))
        put("ple_b_g", _pvec(inp["ple_b_g"][i]))
        for k in range(3):
            put(f"ffn_cw{k}", _pvec(inp["ffn_conv_w"][i][k]))
        put("ffn_cb", _pvec(inp["ffn_conv_b"][i]))
        if i % 2 == 0:
            put("ln_g", _pvec(inp["ev_v_ln_g"][j]))
            put("ln_b", _pvec(inp["ev_v_ln_b"][j]))
            win = np.asarray(inp["ev_w_in"][j], np.float32)
            w[f"wau{i}"] = _chunk_layout(win[:, :1024])
            w[f"wv{i}"] = np.ascontiguousarray(win[:, 1024:].reshape(8, 128, 512).transpose(1, 0, 2)).reshape(128, 4096)
            w[f"wf{i}"] = np.ascontiguousarray(np.asarray(inp["ev_w_fourier"][j], np.float32).transpose(1, 0, 2)).reshape(128, 512)
            w[f"wst{i}"] = np.ascontiguousarray(np.asarray(inp["ev_w_spatial"][j], np.float32).transpose(2, 0, 1)).reshape(128, 512)
            w[f"bs{i}"] = np.ascontiguousarray(np.asarray(inp["ev_b_spatial"][j], np.float32).reshape(1, 512))
            w[f"wo{i}"] = _chunk_layout(np.asarray(inp["ev_w_out"][j]))
        else:
            put("ln_g", _pvec(inp["od_ln_g"][j]))
            put("ln_b", _pvec(inp["od_ln_b"][j]))
            put("conv_b", _pvec(inp["od_conv_b"][j]))
            for k in range(3):
                put(f"sconv_w{k}", _pvec(inp["od_sconv_w"][j][k]))
            cw = np.asarray(inp["od_conv_w"][j], np.float32)
            for k in range(31):
                vecs[:, vb + VEC["conv_w"] + k * 4:vb + VEC["conv_w"] + k * 4 + 4] = _pvec(cw[k])
            w[f"win{i}"] = _chunk_layout(np.asarray(inp["od_w_in"][j]))
            w[f"wo{i}"] = _chunk_layout(np.asarray(inp["od_w_out"][j]))
        wup = np.asarray(inp["ffn_w_up"][i], np.float32)
        wup = wup.reshape(D, 2, NFC, 128).transpose(0, 2, 1, 3).reshape(D, 2 * DFF)
        w[f"wup{i}"] = _chunk_layout(wup)
        w[f"wdn{i}"] = _chunk_layout(np.asarray(inp["ffn_w_down"][i]))
        w[f"wg{i}"] = _chunk_layout(np.asarray(inp["ple_w_g"][i]))
        w[f"wp{i}"] = _chunk_layout(np.asarray(inp["ple_w_p"][i]))
    w["vecs"] = vecs
    return w


FULL_PLAN = [(li, k) for li in range(DEPTH) for k in ("mix", "ffn", "ple")]
_CACHE = {}


def run_plan(plan, inputs, xT_list, cores):
    key = tuple(plan)
    if key not in _CACHE:
        _CACHE[key] = build_program(plan)
    nc = _CACHE[key]
    hw = host_weights(inputs)
    hc = host_consts() if any(k == "mix" for _, k in plan) else {}
    kinds = set(plan)
    shared = {"vecs": hw["vecs"]}
    for name, _ in weight_specs():
        li = int(name[-1])
        kind = "mix" if name[:-1] in ("wau", "wv", "wf", "wst", "wo", "win") else ("ffn" if name[:-1] in ("wup", "wdn") else "ple")
        if (li, kind) in kinds:
            shared[name] = hw[name]
    for li in range(DEPTH):
        if li % 2 == 0 and (li, "mix") in kinds:
            shared[f"bs{li}"] = hw[f"bs{li}"]
    if any(l % 2 == 0 and k == "mix" for l, k in plan):
        shared["dftc"], shared["dfts"], shared["cc"] = hc["dftc"], hc["dfts"], hc["cc"]
    if any(l % 2 == 1 and k == "mix" for l, k in plan):
        shared["ident"] = hc["ident"]
    p = np.asarray(inputs["p"], np.float32)
    in_maps = []
    for ci, b in enumerate(cores):
        m = dict(shared)
        m["xT"] = xT_list[ci]
        m["pT"] = np.ascontiguousarray(p[:, b].transpose(0, 2, 1))
        in_maps.append(m)
    res = run_bass_kernel_spmd(nc, in_maps, core_ids=list(range(len(cores))))
    return [r["outT"] for r in res.results]


def kernel(**inputs):
    x = np.asarray(inputs["x"], np.float32)
    xT = [np.ascontiguousarray(x[b].T) for b in range(NCORES)]
    outs = run_plan(FULL_PLAN, inputs, xT, list(range(NCORES)))
    return np.stack([np.ascontiguousarray(o.T) for o in outs], axis=0).astype(np.float32)
```
